# Optimizing a Trainium2 kernel written in Bass

```python
import math
import jax
import jax.numpy as jnp
from jax import lax
import numpy as np

D_MODEL = 1024
BATCH = 8
SEQ = 4096
DEPTH = 4

EPS = 1e-6
POOL_WINDOWS = (2, 4, 8, 16)
POOL_GROUP = D_MODEL // 8
POOL_WIDTH = len(POOL_WINDOWS) * POOL_GROUP
DIL_PAIRS = ((128, 1), (512, 4), (2048, 16))
DIL_HEADS_PER_GROUP = 4
DIL_HEADS = len(DIL_PAIRS) * DIL_HEADS_PER_GROUP
DIL_HEAD_DIM = 64
DIL_QKV_WIDTH = DIL_HEADS * DIL_HEAD_DIM
DIL_OUT_WIDTH = DIL_HEADS_PER_GROUP * DIL_HEAD_DIM
REL_BUCKETS = 32
REL_MAX_DIST = 2048
SB_HEADS = 4
SB_HEAD_DIM = 128
SB_WIDTH = SB_HEADS * SB_HEAD_DIM
SB_BLOCK = 128
S5_WIDTH = D_MODEL // 2
S5_CH = 16
S5_GROUPS = S5_WIDTH // S5_CH
S5_STATE = 64
FFN_HIDDEN = ((8 * D_MODEL + 3 * 256 - 1) // (3 * 256)) * 256
N_BRANCHES = 4
IN_WIDTH = POOL_WIDTH + 3 * DIL_QKV_WIDTH + 3 * SB_WIDTH + S5_WIDTH
BRANCH_WIDTHS = (POOL_WIDTH, DIL_OUT_WIDTH, SB_WIDTH, S5_WIDTH)
BRANCH_WIDTH = sum(BRANCH_WIDTHS)

kernel_name = 'hybrid_gated_mixer_trunk'


def rmsnorm(x, g):
    xf = x.astype(jnp.float32)
    y = xf * lax.rsqrt(jnp.mean(xf * xf, axis=-1, keepdims=True) + EPS)
    return (y * g.astype(jnp.float32)).astype(x.dtype)


def pool_mixer(u, w_grp, scale):
    B_, S_, _ = u.shape
    uf = u.astype(jnp.float32)
    cs = jnp.concatenate([jnp.zeros_like(uf[:, :1]), jnp.cumsum(uf, axis=1)], axis=1)
    t = jnp.arange(S_)
    outs = []
    for gi, w in enumerate(POOL_WINDOWS):
        sl = slice(gi * POOL_GROUP, (gi + 1) * POOL_GROUP)
        lo = jnp.maximum(t + 1 - w, 0)
        cnt = (t + 1 - lo).astype(jnp.float32)
        mean = (cs[:, 1:, sl] - cs[:, lo, sl]) / cnt[None, :, None]
        outs.append(mean - uf[:, :, sl])
    p = jnp.stack(outs, axis=2)
    y = jnp.einsum('bsgc,gcd->bsgd', p, w_grp.astype(jnp.float32)).reshape(B_, S_, POOL_WIDTH)
    return (y * scale.astype(jnp.float32)).astype(u.dtype)


def t5_bucket(dist):
    exact = REL_BUCKETS // 2
    df = jnp.maximum(dist, 1).astype(jnp.float32)
    large = exact + (jnp.log(df / exact) / math.log(REL_MAX_DIST / exact)
                     * (REL_BUCKETS - exact)).astype(jnp.int32)
    large = jnp.minimum(large, REL_BUCKETS - 1)
    return jnp.where(dist < exact, dist, large)


def dilated_group(q, k, v, bias_tab, window, dil):
    B_, S_, H, E = q.shape
    band = window // dil
    L = S_ // dil
    nb = -(-L // band)
    Lp = nb * band

    def to_sub(a):
        a = a.reshape(B_, L, dil, H, E).transpose(0, 2, 3, 1, 4)
        return jnp.pad(a, ((0, 0), (0, 0), (0, 0), (0, Lp - L), (0, 0)))

    def band_keys(a):
        a = jnp.pad(a, ((0, 0), (0, 0), (0, 0), (band, 0), (0, 0))).reshape(B_, dil, H, nb + 1, band, E)
        return jnp.concatenate([a[:, :, :, :-1], a[:, :, :, 1:]], axis=4)

    qb = to_sub(q).reshape(B_, dil, H, nb, band, E)
    kb = band_keys(to_sub(k))
    vb = band_keys(to_sub(v))
    i = jnp.arange(band)[:, None]
    c = jnp.arange(2 * band)[None, :]
    dist_sub = band + i - c
    in_band = (dist_sub >= 0) & (dist_sub <= band)
    n = jnp.arange(nb)[:, None, None]
    mask = in_band[None] & ((n > 0) | (c[None] >= band))
    buckets = t5_bucket(jnp.clip(dist_sub, 0, band) * dil)
    bias = bias_tab.astype(jnp.float32)[buckets].transpose(2, 0, 1)
    s = jnp.einsum('bdhnqe,bdhnke->bdhnqk', qb, kb, preferred_element_type=jnp.float32) / math.sqrt(E)
    s = jnp.where(mask[None, None, None], s + bias[None, None, :, None], -1e30)
    m = jnp.max(s, axis=-1, keepdims=True)
    p = jnp.exp(s - m)
    den = jnp.sum(p, axis=-1, keepdims=True)
    o = jnp.einsum('bdhnqk,bdhnke->bdhnqe', p, vb.astype(jnp.float32)) / den
    lse = (m + jnp.log(den))[..., 0]
    o = o.reshape(B_, dil, H, Lp, E)[:, :, :, :L].transpose(0, 3, 1, 2, 4).reshape(B_, S_, H, E)
    lse = lse.reshape(B_, dil, H, Lp)[:, :, :, :L].transpose(0, 3, 1, 2).reshape(B_, S_, H)
    return o, lse


def dilated_attention(qkv, rel_bias):
    B_, S_ = qkv.shape[:2]
    outs, lses = [], []
    for g, (window, dil) in enumerate(DIL_PAIRS):
        hs = slice(g * DIL_HEADS_PER_GROUP, (g + 1) * DIL_HEADS_PER_GROUP)
        o, lse = dilated_group(qkv[:, :, 0, hs], qkv[:, :, 1, hs], qkv[:, :, 2, hs],
                               rel_bias[:, hs], window, dil)
        outs.append(o)
        lses.append(lse)
    o = jnp.stack(outs, axis=0)
    alpha = jax.nn.softmax(jnp.stack(lses, axis=0), axis=0)
    y = jnp.sum(alpha[..., None] * o, axis=0)
    return y.reshape(B_, S_, DIL_OUT_WIDTH).astype(qkv.dtype)


def stick_breaking_attention(qkv):
    B_, S_, _, H, E = qkv.shape
    nb = S_ // SB_BLOCK
    qh = qkv[:, :, 0].transpose(0, 2, 1, 3)
    kh = qkv[:, :, 1].transpose(0, 2, 1, 3)
    vh = qkv[:, :, 2].transpose(0, 2, 1, 3).astype(jnp.float32)
    qblk = qh.reshape(B_, H, nb, SB_BLOCK, E).transpose(2, 0, 1, 3, 4)
    key_pos = jnp.arange(S_)

    def one_block(args):
        qb, bi = args
        z = jnp.einsum('bhqe,bhke->bhqk', qb, kh, preferred_element_type=jnp.float32) / math.sqrt(E)
        q_pos = bi * SB_BLOCK + jnp.arange(SB_BLOCK)
        mask = key_pos[None, :] < q_pos[:, None]
        log_not = jnp.where(mask, jax.nn.log_sigmoid(-z), 0.0)
        excl = lax.cumsum(log_not, axis=3, reverse=True) - log_not
        w = jnp.where(mask, jnp.exp(jax.nn.log_sigmoid(z) + excl), 0.0)
        return jnp.einsum('bhqk,bhke->bhqe', w, vh)

    o = lax.map(one_block, (qblk, jnp.arange(nb)))
    return o.transpose(1, 0, 3, 2, 4).reshape(B_, S_, SB_WIDTH).astype(qkv.dtype)


def s5_mixer(u, a_re, a_im, log_dt, b_re, b_im, c_re, c_im, d_skip, w_glu):
    B_, S_, _ = u.shape
    f32 = jnp.float32
    uf = u.astype(f32).reshape(B_, S_, S5_GROUPS, S5_CH)
    lam = lax.complex(a_re.astype(f32), a_im.astype(f32))
    dt = jnp.exp(log_dt.astype(f32))[:, None]
    lam_bar = jnp.exp(lam * dt)
    b_t = lax.complex(b_re.astype(f32), b_im.astype(f32))
    b_bar = ((lam_bar - 1.0) / lam)[:, :, None] * b_t
    bu = jnp.einsum('gpc,bsgc->bsgp', b_bar, uf.astype(jnp.complex64))
    a = jnp.broadcast_to(lam_bar, (1, S_) + lam_bar.shape)

    def combine(left, right):
        a_l, b_l = left
        a_r, b_r = right
        return a_r * a_l, a_r * b_l + b_r

    _, h = lax.associative_scan(combine, (a, bu), axis=1)
    c_t = lax.complex(c_re.astype(f32), c_im.astype(f32))
    y = jnp.real(jnp.einsum('gcp,bsgp->bsgc', c_t, h)) + d_skip.astype(f32).reshape(S5_GROUPS, S5_CH) * uf
    y = jax.nn.gelu(y.reshape(B_, S_, S5_WIDTH))
    gl = y @ w_glu.astype(f32)
    y = gl[..., :S5_WIDTH] * jax.nn.sigmoid(gl[..., S5_WIDTH:])
    return y.astype(u.dtype)


def setup_inputs(seed: int = 0) -> dict:
    key = jax.random.key(seed)
    ks = jax.random.split(key, 22)
    nrm = jax.random.normal
    f32 = jnp.float32
    x = nrm(ks[0], (BATCH, SEQ, D_MODEL), f32)
    attn_norm_g = 1.0 + 0.02 * nrm(ks[1], (DEPTH, D_MODEL), f32)
    w_in = nrm(ks[2], (DEPTH, D_MODEL, IN_WIDTH), f32) * D_MODEL ** -0.5
    pool_w = nrm(ks[3], (DEPTH, len(POOL_WINDOWS), POOL_GROUP, POOL_GROUP), f32) * POOL_GROUP ** -0.5
    pool_scale = 1.0 + 0.02 * nrm(ks[4], (DEPTH, POOL_WIDTH), f32)
    rel_bias = 0.5 * nrm(ks[5], (REL_BUCKETS, DIL_HEADS), f32)
    s5_a_re = -0.5 + 0.01 * nrm(ks[6], (DEPTH, S5_GROUPS, S5_STATE), f32)
    s5_a_im = jnp.pi * jnp.arange(S5_STATE, dtype=f32)[None, None, :] + 0.01 * nrm(ks[7], (DEPTH, S5_GROUPS, S5_STATE), f32)
    s5_log_dt = jax.random.uniform(ks[8], (DEPTH, S5_GROUPS), f32, minval=math.log(1e-3), maxval=math.log(1e-1))
    s5_b_re = nrm(ks[9], (DEPTH, S5_GROUPS, S5_STATE, S5_CH), f32) * (2 * S5_CH) ** -0.5
    s5_b_im = nrm(ks[10], (DEPTH, S5_GROUPS, S5_STATE, S5_CH), f32) * (2 * S5_CH) ** -0.5
    s5_c_re = nrm(ks[11], (DEPTH, S5_GROUPS, S5_CH, S5_STATE), f32) * S5_STATE ** -0.5
    s5_c_im = nrm(ks[12], (DEPTH, S5_GROUPS, S5_CH, S5_STATE), f32) * S5_STATE ** -0.5
    s5_d = nrm(ks[13], (DEPTH, S5_WIDTH), f32)
    s5_w_glu = nrm(ks[14], (DEPTH, S5_WIDTH, 2 * S5_WIDTH), f32) * S5_WIDTH ** -0.5
    row_scale = jnp.concatenate([jnp.full((w,), w ** -0.5, f32) for w in BRANCH_WIDTHS])
    w_branch = nrm(ks[15], (DEPTH, BRANCH_WIDTH, D_MODEL), f32) * row_scale[None, :, None]
    w_gate = nrm(ks[16], (DEPTH, N_BRANCHES, D_MODEL, D_MODEL), f32) * D_MODEL ** -0.5
    w_out = nrm(ks[17], (DEPTH, D_MODEL, D_MODEL), f32) * D_MODEL ** -0.5
    ffn_norm_g = 1.0 + 0.02 * nrm(ks[18], (DEPTH, D_MODEL), f32)
    w_up = nrm(ks[19], (DEPTH, D_MODEL, 2 * FFN_HIDDEN), f32) * D_MODEL ** -0.5
    w_down = nrm(ks[20], (DEPTH, FFN_HIDDEN, D_MODEL), f32) * FFN_HIDDEN ** -0.5
    final_norm_g = 1.0 + 0.02 * nrm(ks[21], (D_MODEL,), f32)
    return {'x': x, 'attn_norm_g': attn_norm_g, 'w_in': w_in, 'pool_w': pool_w,
            'pool_scale': pool_scale, 'rel_bias': rel_bias, 's5_a_re': s5_a_re,
            's5_a_im': s5_a_im, 's5_log_dt': s5_log_dt, 's5_b_re': s5_b_re, 's5_b_im': s5_b_im,
            's5_c_re': s5_c_re, 's5_c_im': s5_c_im, 's5_d': s5_d, 's5_w_glu': s5_w_glu,
            'w_branch': w_branch, 'w_gate': w_gate, 'w_out': w_out, 'ffn_norm_g': ffn_norm_g,
            'w_up': w_up, 'w_down': w_down, 'final_norm_g': final_norm_g}


def reference(x, attn_norm_g, w_in, pool_w, pool_scale, rel_bias, s5_a_re, s5_a_im, s5_log_dt,
              s5_b_re, s5_b_im, s5_c_re, s5_c_im, s5_d, s5_w_glu, w_branch, w_gate, w_out,
              ffn_norm_g, w_up, w_down, final_norm_g):
    B_, S_, _ = x.shape
    o1 = POOL_WIDTH
    o2 = o1 + 3 * DIL_QKV_WIDTH
    o3 = o2 + 3 * SB_WIDTH
    for l in range(DEPTH):
        xn = rmsnorm(x, attn_norm_g[l])
        proj = xn @ w_in[l]
        u_pool = proj[..., :o1]
        qkv_dil = proj[..., o1:o2].reshape(B_, S_, 3, DIL_HEADS, DIL_HEAD_DIM)
        qkv_sb = proj[..., o2:o3].reshape(B_, S_, 3, SB_HEADS, SB_HEAD_DIM)
        u_s5 = proj[..., o3:]
        y_pool = pool_mixer(u_pool, pool_w[l], pool_scale[l])
        y_dil = dilated_attention(qkv_dil, rel_bias)
        y_sb = stick_breaking_attention(qkv_sb)
        y_s5 = s5_mixer(u_s5, s5_a_re[l], s5_a_im[l], s5_log_dt[l], s5_b_re[l], s5_b_im[l],
                        s5_c_re[l], s5_c_im[l], s5_d[l], s5_w_glu[l])
        merged = jnp.zeros_like(x)
        row = 0
        for bi, yb in enumerate((y_pool, y_dil, y_sb, y_s5)):
            width = BRANCH_WIDTHS[bi]
            gate = jax.nn.sigmoid(xn @ w_gate[l, bi])
            merged = merged + gate * (yb @ w_branch[l, row:row + width])
            row += width
        x = x + merged @ w_out[l]
        hn = rmsnorm(x, ffn_norm_g[l])
        gu = hn @ w_up[l]
        h = jax.nn.silu(gu[..., :FFN_HIDDEN]) * gu[..., FFN_HIDDEN:]
        x = x + h @ w_down[l]
    return rmsnorm(x, final_norm_g)
```

```python
import math
import numpy as np
from contextlib import ExitStack
import concourse.bass as bass
import concourse.mybir as mybir
from concourse.bass_utils import run_bass_kernel_spmd

F32 = mybir.dt.float32
BF16 = mybir.dt.bfloat16
AF = mybir.ActivationFunctionType
ALU = mybir.AluOpType

S = 4096
D = 1024
NL = 4
NT = 32
NS = 8
FH = 2816
EPS = 1e-6
NEG = -30000.0
DILS = (1, 4, 16)


class Buf:
    __slots__ = ("name", "w", "rs")

    def __init__(self, name=""):
        self.name = name
        self.w = None
        self.rs = []


class Prog:
    CE = ("pe", "act", "dve", "pool")

    def __init__(self, nc, es):
        self.nc = nc
        self.es = es
        self.ops = {e: [] for e in ("pe", "act", "dve", "pool", "sp")}
        self.esem = {e: es.enter_context(nc.semaphore("es_" + e)) for e in self.CE}
        self.ecnt = {e: 0 for e in self.CE}
        self.dsems = []

    def dma_sem(self, name):
        s = self.es.enter_context(self.nc.semaphore("ds_%s_%d" % (name, len(self.dsems))))
        d = {"s": s, "n": 0}
        self.dsems.append(d)
        return d

    def _waits(self, reads, writes, extra):
        toks = list(extra)
        for b in reads:
            if b.w is not None:
                toks.append(b.w)
        for b in writes:
            if b.w is not None:
                toks.append(b.w)
            toks.extend(b.rs)
        return toks

    def _commit(self, tok, reads, writes):
        for b in reads:
            b.rs.append(tok)
            if len(b.rs) > 64:
                b.rs = b.rs[-64:]
        for b in writes:
            b.w = tok
            b.rs = []

    def op(self, eng, fns, reads=(), writes=(), waits=()):
        if not isinstance(fns, (list, tuple)):
            fns = [fns]
        toks = self._waits(reads, writes, waits)
        self.ecnt[eng] += 1
        tok = (self.esem[eng], self.ecnt[eng])
        self.ops[eng].append((list(fns), toks, (self.esem[eng], 1)))
        self._commit(tok, reads, writes)
        return tok

    def dma(self, eng, out, in_, sem, reads=(), writes=(), waits=(), **kw):
        toks = self._waits(reads, writes, waits)
        sem["n"] += 16
        tok = (sem["s"], sem["n"])
        self.ops[eng].append(([lambda e: e.dma_start(out=out, in_=in_, **kw)], toks, (sem["s"], 16)))
        self._commit(tok, reads, writes)
        return tok

    def all_tokens(self):
        toks = [(self.esem[e], self.ecnt[e]) for e in self.CE if self.ecnt[e] > 0]
        toks += [(d["s"], d["n"]) for d in self.dsems if d["n"] > 0]
        return toks

    def barrier(self):
        toks = self.all_tokens()
        for e in self.ops:
            self.ops[e].append(([], list(toks), None))

    def _replay(self, eng, e):
        waited = {}
        for fns, toks, inc in self.ops[eng]:
            need = {}
            for s, v in toks:
                k = id(s)
                if waited.get(k, 0) >= v:
                    continue
                if k not in need or need[k][1] < v:
                    need[k] = (s, v)
            for k, (s, v) in need.items():
                e.wait_ge(s, v)
                waited[k] = v
            ins = None
            for f in fns:
                ins = f(e)
            if inc is not None and ins is not None:
                ins.then_inc(inc[0], inc[1])

    def build(self):
        with self.nc.Block() as block:
            @block.tensor
            def _(e):
                self._replay("pe", e)

            @block.scalar
            def _(e):
                self._replay("act", e)

            @block.vector
            def _(e):
                self._replay("dve", e)

            @block.gpsimd
            def _(e):
                self._replay("pool", e)

            @block.sync
            def _(e):
                self._replay("sp", e)


class Arena:
    def __init__(self, t, ncols):
        self.t = t
        self.n = ncols
        self.p = 0

    def reset(self):
        self.p = 0

    def f32(self, cols, shape=None):
        a = self.p
        self.p += cols
        assert self.p <= self.n, "arena overflow %d > %d" % (self.p, self.n)
        v = self.t[:, a:a + cols]
        return v

    def bf16(self, cols):
        c32 = (cols + 1) // 2
        a = self.p
        self.p += c32
        assert self.p <= self.n, "arena overflow %d > %d" % (self.p, self.n)
        return self.t[:, a:a + c32].bitcast(BF16)[:, 0:cols]


def r3(ap, a):
    return ap.rearrange("p (a b) -> p a b", a=a)


def _t5_bucket_np(dist):
    exact = 16
    df = np.maximum(dist, 1).astype(np.float32)
    large = exact + (np.log(df / np.float32(exact)) / np.float32(math.log(2048 / exact))
                     * np.float32(32 - exact)).astype(np.int32)
    large = np.minimum(large, 31)
    return np.where(dist < exact, dist, large)


def dil_block_list():
    bl = []
    for g, d in enumerate(DILS):
        for delta in range(d + 1):
            bl.append((g, delta))
    return bl


def build_dil_tables(rel_bias):
    bl = dil_block_list()
    c = np.arange(128)[:, None]
    i = np.arange(128)[None, :]
    out = np.empty((4, len(bl), 128, 128), np.float32)
    for j in range(4):
        for bi, (g, delta) in enumerate(bl):
            d = DILS[g]
            dist = 128 * delta + i - c
            valid = (dist >= 0) & (dist % d == 0) & (dist <= 128 * d)
            bucket = _t5_bucket_np(np.clip(dist, 0, 128 * d))
            vals = rel_bias[bucket, 4 * g + j]
            out[j, bi] = np.where(valid, vals, np.float32(NEG))
    return out


def build_consts():
    c = {}
    c["ident"] = np.eye(128, dtype=np.float32)
    j = np.arange(128)[:, None]
    s = np.arange(128)[None, :]
    c["negU"] = np.where(j >= s, -1.0, 0.0).astype(np.float32)
    t = np.arange(512)[None, :]
    c["sbmask"] = np.stack([((128 * d + j) < t) for d in range(4)], 0).astype(np.float32)
    c["iota"] = np.broadcast_to(np.arange(128, dtype=np.float32)[None, :], (128, 128)).copy()
    c["iota512"] = np.broadcast_to(np.arange(512, dtype=np.float32)[None, :], (128, 512)).copy()
    inv = np.zeros((4, 16), np.float32)
    for gi, w in enumerate((2, 4, 8, 16)):
        tt = np.arange(16)
        inv[gi] = 1.0 / np.minimum(tt + 1, w)
    c["poolinv"] = np.broadcast_to(inv.reshape(1, 64), (128, 64)).copy()
    return c


def s5_layouts(inp):
    o = {}
    def st(a):
        return np.ascontiguousarray(a.reshape(4, 16, 2, 64).transpose(0, 2, 3, 1).reshape(4, 128, 16))
    o["s5are"] = st(inp["s5_a_re"])
    o["s5aim"] = st(inp["s5_a_im"])
    ldt = np.broadcast_to(inp["s5_log_dt"][:, :, None], (4, 32, 64))
    o["s5ldt"] = st(np.ascontiguousarray(ldt))
    def bpad(b):
        out = np.zeros((4, 16, 128, 128), np.float32)
        for g in range(32):
            jj, g2 = g // 2, g % 2
            gc = g % 8
            out[:, jj, g2 * 64:(g2 + 1) * 64, gc * 16:(gc + 1) * 16] = b[:, g]
        return out
    o["s5bre"] = bpad(inp["s5_b_re"])
    o["s5bim"] = bpad(inp["s5_b_im"])
    def cpad(c):
        out = np.zeros((4, 16, 128, 128), np.float32)
        for g in range(32):
            jj, g2 = g // 2, g % 2
            gc = g % 8
            out[:, jj, g2 * 64:(g2 + 1) * 64, gc * 16:(gc + 1) * 16] = c[:, g].transpose(0, 2, 1)
        return out
    o["s5cre"] = cpad(inp["s5_c_re"])
    o["s5cim"] = cpad(inp["s5_c_im"])
    return o


class K:
    pass


def build_program(nlayers=NL, debug=False, stages=None):
    nc = bass.Bass("TRN2", target_bir_lowering=False)
    k = K()
    k.nc = nc
    k.debug = debug
    dt_in = lambda name, shape: nc.dram_tensor(name, list(shape), F32, kind="ExternalInput").ap()
    I = {}
    I["x"] = dt_in("x", [S, D])
    I["attn_norm_g"] = dt_in("attn_norm_g", [NL, D])
    I["w_in"] = dt_in("w_in", [NL, D, 4864])
    I["pool_w"] = dt_in("pool_w", [NL, 4, 128, 128])
    I["pool_scale"] = dt_in("pool_scale", [NL, 512])
    I["s5_d"] = dt_in("s5_d", [NL, 512])
    I["s5_w_glu"] = dt_in("s5_w_glu", [NL, 512, 1024])
    I["w_branch"] = dt_in("w_branch", [NL, 1792, D])
    I["w_gate"] = dt_in("w_gate", [NL, 4, D, D])
    I["w_out"] = dt_in("w_out", [NL, D, D])
    I["ffn_norm_g"] = dt_in("ffn_norm_g", [NL, D])
    I["w_up"] = dt_in("w_up", [NL, D, 2 * FH])
    I["w_down"] = dt_in("w_down", [NL, FH, D])
    I["final_norm_g"] = dt_in("final_norm_g", [D])
    I["dil_tab"] = dt_in("dil_tab", [4, 24, 128, 128])
    I["ident"] = dt_in("ident", [128, 128])
    I["negU"] = dt_in("negU", [128, 128])
    I["sbmask"] = dt_in("sbmask", [4, 128, 512])
    I["iota"] = dt_in("iota", [128, 128])
    I["iota512"] = dt_in("iota512", [128, 512])
    I["poolinv"] = dt_in("poolinv", [128, 64])
    for nm in ("s5are", "s5aim", "s5ldt"):
        I[nm] = dt_in(nm, [NL, 128, 16])
    for nm in ("s5bre", "s5bim", "s5cre", "s5cim"):
        I[nm] = dt_in(nm, [NL, 16, 128, 128])
    k.I = I
    kind_dbg = "ExternalOutput" if debug else "Internal"
    O = {}
    O["out"] = nc.dram_tensor("out", [S, D], F32, kind="ExternalOutput").ap()
    scr = lambda name, shape, dt: nc.dram_tensor(name, list(shape), dt, kind=kind_dbg).ap()
    O["xTa"] = scr("xTa", [D, S], F32)
    O["xTb"] = scr("xTb", [D, S], F32)
    O["upool"] = scr("upool", [512, S], F32)
    O["qkT"] = scr("qkT", [3072, S], BF16)
    O["gate"] = scr("gate", [NS, 128, 32, 512], BF16)
    O["vdil"] = scr("vdil", [12, 128, NT * 64], BF16)
    O["vsb"] = scr("vsb", [4, 128, NT * 128], BF16)
    O["ybrT"] = scr("ybrT", [NS, 128, 14, 512], BF16)
    O["hT"] = scr("hT", [FH, S], BF16)
    k.O = O

    with ExitStack() as es:
        P = Prog(nc, es)
        k.P = P
        arena_cols = 50 * 1024
        arena_t = es.enter_context(nc.sbuf_tensor("arena", [128, arena_cols], F32))
        k.A = Arena(arena_t, arena_cols)
        k.pb = [es.enter_context(nc.psum_tensor("pb%d" % i, [128, 512], F32)) for i in range(8)]
        k.pbB = [Buf("pb%d" % i) for i in range(8)]
        k.DB = {n: Buf(n) for n in O}
        k.sems = {}

        def sem(name):
            if name not in k.sems:
                k.sems[name] = P.dma_sem(name)
            return k.sems[name]
        k.sem = sem

        stage_consts(k)
        cur, nxt = "xTa", "xTb"
        if stages is None or "load" in stages:
            stage_load_x(k, cur)
        for l in range(nlayers):
            if stages is None or "A" in stages:
                stage_A(k, l, cur)
            if stages is None or "pool" in stages:
                stage_pool(k, l)
            if stages is None or "dil" in stages:
                stage_dil(k, l)
            if stages is None or "sb" in stages:
                stage_sb(k, l)
            if stages is None or "s5" in stages:
                stage_s5(k, l)
            if stages is None or "merge" in stages:
                stage_merge(k, l, cur, nxt)
                cur, nxt = nxt, cur
            if stages is None or "ffn" in stages:
                stage_ffn(k, l, cur, nxt)
                cur, nxt = nxt, cur
        if stages is None or "final" in stages:
            stage_final(k, cur)
        P.barrier()
        P.build()
    return nc


def stage_consts(k):
    P, A, I = k.P, k.A, k.I
    k.ident_f = A.f32(128)
    k.ident_b = A.bf16(128)
    k.ones_b = A.bf16(128)
    k.negU_b = A.bf16(128)
    k.negones_b = A.bf16(128)
    k.iota_f = A.f32(128)
    k.negones_f = A.f32(128)
    k.negident_b = A.bf16(128)
    k.gcols = A.f32(8 * 9)
    k.CB = Buf("consts")
    s = k.sem("const")
    P.dma("sp", k.ident_f, I["ident"], s, writes=[k.CB])
    P.dma("pool", k.ident_b, I["ident"], k.sem("const2"), writes=[k.CB])
    P.dma("pool", k.negU_b, I["negU"], k.sem("const2"), writes=[k.CB])
    P.dma("sp", k.iota_f, I["iota"], s, writes=[k.CB])
    gv = r3(k.gcols, 9)
    for l in range(NL):
        P.dma("sp", gv[:, l, :], I["attn_norm_g"][l].rearrange("(kc p) -> p kc", p=128), s, writes=[k.CB], allow_slow_non_contiguous=True)
        P.dma("sp", gv[:, 4 + l, :], I["ffn_norm_g"][l].rearrange("(kc p) -> p kc", p=128), s, writes=[k.CB], allow_slow_non_contiguous=True)
    P.dma("sp", gv[:, 8, :], I["final_norm_g"].rearrange("(kc p) -> p kc", p=128), s, writes=[k.CB], allow_slow_non_contiguous=True)
    P.op("dve", lambda e: e.memset(k.ones_b, 1.0), writes=[k.CB])
    P.op("dve", lambda e: e.memset(k.negones_b, -1.0), writes=[k.CB])
    P.op("dve", lambda e: e.memset(k.negones_f, -1.0), writes=[k.CB])
    P.op("dve", lambda e: e.tensor_scalar_mul(out=k.negident_b, in0=k.ident_b, scalar1=-1.0), writes=[k.CB])
    k.const_end = A.p
    P.barrier()


def psum_rot(k, idxs):
    state = {"i": 0}

    def nxt():
        i = idxs[state["i"] % len(idxs)]
        state["i"] += 1
        return k.pb[i], k.pbB[i]
    return nxt


def rmsnorm_supertile(k, xs, xsB, gidx, xn_out, xn_B, sq, sqB, rt, rtB, bank, out_engines=("dve",)):
    P = k.P
    pbt, pbB = bank
    gv = r3(k.gcols, 9)
    P.op("act", lambda e: e.activation(out=sq, in_=xs, func=AF.Square), reads=[xsB], writes=[sqB])
    fns = []
    for kc in range(8):
        fns.append(lambda e, kc=kc: e.matmul(pbt[:, :], lhsT=k.ones_b, rhs=sq[:, kc, :], start=(kc == 0), stop=(kc == 7)))
    P.op("pe", fns, reads=[sqB, k.CB], writes=[pbB])
    P.op("act", lambda e: e.activation(out=rt, in_=pbt[:, :], func=AF.Sqrt, bias=EPS, scale=1.0 / D), reads=[pbB], writes=[rtB])
    P.op("dve", lambda e: e.reciprocal(out=rt, in_=rt), reads=[rtB], writes=[rtB])
    for kc in range(8):
        eng = out_engines[kc % len(out_engines)]
        P.op(eng, lambda e, kc=kc: e.scalar_tensor_tensor(out=xn_out[:, kc, :], in0=xs[:, kc, :], scalar=gv[:, gidx, kc:kc + 1],
                                                         in1=rt, op0=ALU.mult, op1=ALU.mult),
             reads=[xsB, rtB, k.CB], writes=[xn_B])


def stage_load_x(k, cur):
    P, A, I, O = k.P, k.A, k.I, k.O
    A.p = k.const_end
    xin = [A.f32(1024) for _ in range(2)]
    xinB = [Buf("xin%d" % i) for i in range(2)]
    xs = [r3(A.f32(4096), 8) for _ in range(2)]
    xsB = [Buf("xs%d" % i) for i in range(2)]
    rot = psum_rot(k, [0, 1, 2, 3])
    dst = O[cur].rearrange("(kc p) t -> p kc t", p=128)
    for s in range(NS):
        for tt in range(4):
            t = s * 4 + tt
            xi, xiB = xin[t % 2], xinB[t % 2]
            P.dma("sp", xi, I["x"][t * 128:(t + 1) * 128, :], k.sem("lx%d" % (t % 2)), writes=[xiB])
            for half in range(2):
                pbt, pbB = rot()
                fns = [lambda e, q=q, pbt=pbt, xi=xi, half=half: e.transpose(out=pbt[:, q * 128:(q + 1) * 128],
                                                                             in_=xi[:, (half * 4 + q) * 128:(half * 4 + q + 1) * 128],
                                                                             identity=k.ident_f) for q in range(4)]
                P.op("pe", fns, reads=[xiB, k.CB], writes=[pbB])
                eng = "dve" if half == 0 else "act"
                dstv = xs[s % 2][:, half * 4:(half + 1) * 4, tt * 128:(tt + 1) * 128]
                srcv = r3(pbt[:, :], 4)
                if eng == "dve":
                    P.op("dve", lambda e, dstv=dstv, srcv=srcv: e.tensor_copy(out=dstv, in_=srcv), reads=[pbB], writes=[xsB[s % 2]])
                else:
                    P.op("act", lambda e, dstv=dstv, srcv=srcv: e.activation(out=dstv, in_=srcv, func=AF.Copy), reads=[pbB], writes=[xsB[s % 2]])
        P.dma("sp", dst[:, :, s * 512:(s + 1) * 512], xs[s % 2], k.sem("sx%d" % (s % 2)), reads=[xsB[s % 2]], writes=[k.DB[cur]])
    P.barrier()


def stage_A(k, l, cur):
    P, A, I, O = k.P, k.A, k.I, k.O
    A.p = k.const_end
    xnT = r3(A.bf16(8 * S), 8)
    xnB = [Buf("xn%d" % s) for s in range(NS)]
    p_after_xn = A.p
    xs = [r3(A.f32(4096), 8) for _ in range(2)]
    xsB = [Buf("xs%d" % i) for i in range(2)]
    sq = r3(A.bf16(4096), 8)
    sqB = Buf("sq")
    rt = A.f32(512)
    rtB = Buf("rt")
    src = O[cur].rearrange("(kc p) t -> p kc t", p=128)
    for s in range(NS):
        P.dma("sp", xs[s % 2], src[:, :, s * 512:(s + 1) * 512], k.sem("ax%d" % (s % 2)), reads=[k.DB[cur]], writes=[xsB[s % 2]])
        rmsnorm_supertile(k, xs[s % 2], xsB[s % 2], l, xnT[:, :, s * 512:(s + 1) * 512], xnB[s], sq, sqB, rt, rtB, (k.pb[7], k.pbB[7]))
    w_in = I["w_in"][l]
    jobs = []
    jobs.append((w_in[:, 0:512], O["upool"], "f32", 1.0, "upool"))
    jobs.append((w_in[:, 512:1280], O["qkT"][0:768], "bf", 0.125, "qkT"))
    jobs.append((w_in[:, 1280:2048], O["qkT"][768:1536], "bf", 1.0, "qkT"))
    jobs.append((w_in[:, 2816:3328], O["qkT"][1536:2048], "bf", 128.0 ** -0.5, "qkT"))
    jobs.append((w_in[:, 3328:3840], O["qkT"][2048:2560], "bf", 1.0, "qkT"))
    jobs.append((w_in[:, 4352:4864], O["qkT"][2560:3072], "bf", 1.0, "qkT"))
    for b in range(4):
        jobs.append((I["w_gate"][l, b], b * 8, "sig", 1.0, "gate"))
    wsl = [r3(A.bf16(8 * 512), 8) for _ in range(2)]
    wslB = [Buf("wsl%d" % i) for i in range(2)]
    ot_b = [A.bf16(S) for _ in range(2)]
    ot_f = A.f32(S)
    otB = [Buf("ot%d" % i) for i in range(3)]
    rot = psum_rot(k, [0, 1, 2, 3, 4, 5])
    slab_i = 0
    out_i = 0
    ev_i = 0
    for (wap, dest, mode, scale, dname) in jobs:
        ncols = wap.shape[1]
        wv = wap.rearrange("(kc p) n -> p kc n", p=128)
        for c0 in range(0, ncols, 512):
            cw = min(512, ncols - c0)
            sl, slB = wsl[slab_i % 2], wslB[slab_i % 2]
            P.dma("pool", sl[:, :, 0:cw], wv[:, :, c0:c0 + cw], k.sem("wsl%d" % (slab_i % 2)), writes=[slB])
            slab_i += 1
            for cc in range(cw // 128):
                if mode == "f32":
                    ot, oB, osem = ot_f, otB[2], "ot2"
                else:
                    ot, oB, osem = ot_b[out_i % 2], otB[out_i % 2], "ot%d" % (out_i % 2)
                    out_i += 1
                for s in range(NS):
                    pbt, pbB = rot()
                    fns = [lambda e, kc=kc, pbt=pbt, sl=sl, cc=cc, s=s: e.matmul(pbt[:, :], lhsT=sl[:, kc, cc * 128:(cc + 1) * 128],
                                                                                rhs=xnT[:, kc, s * 512:(s + 1) * 512],
                                                                                start=(kc == 0), stop=(kc == 7)) for kc in range(8)]
                    P.op("pe", fns, reads=[slB, xnB[s]], writes=[pbB])
                    ov = ot[:, s * 512:(s + 1) * 512]
                    if mode == "sig":
                        P.op("act", lambda e, ov=ov, pbt=pbt: e.activation(out=ov, in_=pbt[:, :], func=AF.Sigmoid), reads=[pbB], writes=[oB])
                    else:
                        if ev_i % 2 == 0:
                            P.op("act", lambda e, ov=ov, pbt=pbt, scale=scale: e.activation(out=ov, in_=pbt[:, :], func=AF.Copy, scale=scale), reads=[pbB], writes=[oB])
                        else:
                            P.op("dve", lambda e, ov=ov, pbt=pbt, scale=scale: e.tensor_scalar(out=ov, in0=pbt[:, :], scalar1=scale, scalar2=None, op0=ALU.mult), reads=[pbB], writes=[oB])
                        ev_i += 1
                r0 = c0 + cc * 128
                if mode == "sig":
                    rch = dest + r0 // 128
                    P.dma("sp", O["gate"][:, :, rch, :].rearrange("s p t -> p s t"), ot.rearrange("p (s t) -> p s t", s=NS), k.sem(osem), reads=[oB], writes=[k.DB[dname]])
                else:
                    P.dma("sp", dest[r0:r0 + 128, :], ot, k.sem(osem), reads=[oB], writes=[k.DB[dname]])
    P.barrier()
    A.p = p_after_xn
    wv_sb = r3(A.bf16(8 * 1280), 8)
    wvB = Buf("wv")
    wview = w_in.rearrange("(kc p) n -> p kc n", p=128)
    P.dma("pool", wv_sb[:, :, 0:768], wview[:, :, 2048:2816], k.sem("wv"), writes=[wvB])
    P.dma("pool", wv_sb[:, :, 768:1280], wview[:, :, 3840:4352], k.sem("wv"), writes=[wvB])
    vd = A.bf16(12 * NT * 64).rearrange("p (h t e) -> p h t e", h=12, t=NT)
    vs = A.bf16(4 * NT * 128).rearrange("p (h t e) -> p h t e", h=4, t=NT)
    vallB = Buf("vall")
    for t in range(NT):
        s = t // 4
        for gi, (n0, n1) in enumerate(((0, 512), (512, 768), (768, 1280))):
            pbt, pbB = rot()
            fns = [lambda e, kc=kc, pbt=pbt, n0=n0, n1=n1, t=t: e.matmul(pbt[:, 0:n1 - n0], lhsT=xnT[:, kc, t * 128:(t + 1) * 128],
                                                                         rhs=wv_sb[:, kc, n0:n1], start=(kc == 0), stop=(kc == 7)) for kc in range(8)]
            P.op("pe", fns, reads=[wvB, xnB[s]], writes=[pbB])
            if gi == 0:
                ov, iv = vd[:, 0:8, t, :], pbt[:, 0:512].rearrange("p (h e) -> p h e", h=8)
                P.op("dve", lambda e, ov=ov, iv=iv: e.tensor_copy(out=ov, in_=iv), reads=[pbB], writes=[vallB])
            elif gi == 1:
                ov, iv = vd[:, 8:12, t, :], pbt[:, 0:256].rearrange("p (h e) -> p h e", h=4)
                P.op("act", lambda e, ov=ov, iv=iv: e.activation(out=ov, in_=iv, func=AF.Copy), reads=[pbB], writes=[vallB])
            else:
                ov, iv = vs[:, 0:4, t, :], pbt[:, 0:512].rearrange("p (h e) -> p h e", h=4)
                P.op("act", lambda e, ov=ov, iv=iv: e.activation(out=ov, in_=iv, func=AF.Copy), reads=[pbB], writes=[vallB])
    for h in range(12):
        P.dma("sp", O["vdil"][h], vd[:, h].rearrange("p t e -> p (t e)"), k.sem("vst"), reads=[vallB], writes=[k.DB["vdil"]])
    for h in range(4):
        P.dma("sp", O["vsb"][h], vs[:, h].rearrange("p t e -> p (t e)"), k.sem("vst"), reads=[vallB], writes=[k.DB["vsb"]])
    P.barrier()


def stage_pool(k, l):
    P, A, I, O = k.P, k.A, k.I, k.O
    A.p = k.const_end
    PADC = 16
    pw = r3(A.bf16(4 * 128), 4)
    pwB = Buf("pw")
    P.dma("pool", pw, I["pool_w"][l].rearrange("g c d -> c g d"), k.sem("pw"), writes=[pwB])
    psc = A.f32(4)
    pinv = A.f32(64)
    P.dma("sp", psc, I["pool_scale"][l].rearrange("(g p) -> p g", p=128), k.sem("pw2"), writes=[pwB], allow_slow_non_contiguous=True)
    P.dma("sp", pinv, I["poolinv"], k.sem("pw2"), writes=[pwB])
    u = [A.f32(PADC + S) for _ in range(2)]
    uB = [Buf("u%d" % i) for i in range(2)]
    s1 = A.f32(PADC + S)
    s2 = A.f32(PADC + S)
    sB = [Buf("s1"), Buf("s2")]
    pb16 = A.bf16(S)
    pb16B = Buf("pb16")
    yo = [A.bf16(S) for _ in range(2)]
    yoB = [Buf("yo%d" % i) for i in range(2)]
    rot = psum_rot(k, [0, 1, 2, 3])
    for i in range(2):
        P.op("dve", lambda e, i=i: e.memset(u[i][:, 0:PADC], 0.0), writes=[uB[i]])
    P.op("dve", lambda e: e.memset(s1[:, 0:PADC], 0.0), writes=[sB[0]])
    P.op("dve", lambda e: e.memset(s2[:, 0:PADC], 0.0), writes=[sB[1]])
    for g, w in enumerate((2, 4, 8, 16)):
        ug, ugB = u[g % 2], uB[g % 2]
        P.dma("sp", ug[:, PADC:], O["upool"][g * 128:(g + 1) * 128, :], k.sem("pu%d" % (g % 2)), reads=[k.DB["upool"]], writes=[ugB])
        src, srcB = ug, ugB
        sh = 1
        bufs = [(s1, sB[0]), (s2, sB[1])]
        bi = 0
        while sh < w:
            dst, dstB = bufs[bi % 2]
            eng = "dve"
            P.op(eng, lambda e, dst=dst, src=src, sh=sh: e.tensor_add(out=dst[:, PADC:], in0=src[:, PADC:], in1=src[:, PADC - sh:PADC + S - sh]),
                 reads=[srcB], writes=[dstB])
            src, srcB = dst, dstB
            bi += 1
            sh *= 2
        dst, dstB = bufs[bi % 2]
        P.op("dve", lambda e, dst=dst, src=src, ug=ug, w=w: e.scalar_tensor_tensor(out=dst[:, PADC:], in0=src[:, PADC:], scalar=1.0 / w, in1=ug[:, PADC:],
                                                                            op0=ALU.mult, op1=ALU.subtract),
             reads=[srcB, ugB], writes=[dstB])
        tmpc = A.f32(16)
        tB = Buf("tmpc")
        P.op("dve", lambda e, src=src, g=g, tmpc=tmpc: e.tensor_mul(out=tmpc, in0=src[:, PADC:PADC + 16], in1=pinv[:, g * 16:(g + 1) * 16]),
             reads=[srcB, pwB], writes=[tB])
        P.op("dve", lambda e, dst=dst, ug=ug, tmpc=tmpc: e.tensor_sub(out=dst[:, PADC:PADC + 16], in0=tmpc, in1=ug[:, PADC:PADC + 16]),
             reads=[tB, ugB], writes=[dstB])
        P.op("act", lambda e, dst=dst: e.activation(out=pb16, in_=dst[:, PADC:], func=AF.Copy), reads=[dstB], writes=[pb16B])
        y_t, y_B = yo[g % 2], yoB[g % 2]
        for s in range(NS):
            pbt, pbB = rot()
            P.op("pe", lambda e, pbt=pbt, g=g, s=s: e.matmul(pbt[:, :], lhsT=pw[:, g, :], rhs=pb16[:, s * 512:(s + 1) * 512], start=True, stop=True),
                 reads=[pwB, pb16B], writes=[pbB])
            P.op("dve", lambda e, pbt=pbt, g=g, s=s, y_t=y_t: e.tensor_scalar(out=y_t[:, s * 512:(s + 1) * 512], in0=pbt[:, :], scalar1=psc[:, g:g + 1], scalar2=None, op0=ALU.mult),
                 reads=[pbB, pwB], writes=[y_B])
        P.dma("sp", O["ybrT"][:, :, g, :].rearrange("s p t -> p s t"), y_t.rearrange("p (s t) -> p s t", s=NS), k.sem("py%d" % (g % 2)), reads=[y_B], writes=[k.DB["ybrT"]])
    P.barrier()


def stage_dil(k, l):
    P, A, I, O = k.P, k.A, k.I, k.O
    A.p = k.const_end
    bl = dil_block_list()
    nb = len(bl)
    tab = r3(A.bf16(nb * 128), nb)
    tabB = Buf("tab")
    qT = r3(A.bf16(3 * S), 3)
    kT = r3(A.bf16(3 * S), 3)
    qkB = Buf("qk")
    vv = A.bf16(NT * 3 * 64).rearrange("p (g t e) -> p g t e", t=NT, g=3)
    vB = Buf("v")
    pm = [A.bf16(512) for _ in range(3)]
    pmB = [Buf("pm%d" % i) for i in range(3)]
    rd = A.f32(128)
    rdB = Buf("rd")
    yd = [A.bf16(S) for _ in range(2)]
    ydB = [Buf("yd%d" % i) for i in range(2)]
    srot = psum_rot(k, [0, 1, 2])
    orot = psum_rot(k, [3, 4, 5, 6])
    for j in range(4):
        P.dma("pool", tab, I["dil_tab"][j].rearrange("b c i -> c b i"), k.sem("dtab"), writes=[tabB])
        for g in range(3):
            h = 4 * g + j
            P.dma("sp", qT[0:64, g, :], O["qkT"][h * 64:(h + 1) * 64, :], k.sem("dq"), reads=[k.DB["qkT"]], writes=[qkB])
            P.dma("sp", kT[0:64, g, :], O["qkT"][768 + h * 64:768 + (h + 1) * 64, :], k.sem("dq"), reads=[k.DB["qkT"]], writes=[qkB])
            P.dma("sp", vv[:, g].rearrange("p t e -> p (t e)"), O["vdil"][h], k.sem("dv"), reads=[k.DB["vdil"]], writes=[vB])
        y_t, y_B = yd[j % 2], ydB[j % 2]
        items = []
        for n in range(NT):
            blocks = [(bi, g, n - delta) for bi, (g, delta) in enumerate(bl) if n - delta >= 0]
            nblk = len(blocks)
            ngrp = (nblk + 3) // 4
            for gi_, c0 in enumerate(range(0, nblk, 4)):
                items.append((n, gi_, ngrp, nblk, c0, blocks[c0:c0 + 4]))
        obanks = [((k.pb[3], k.pbB[3]), (k.pb[4], k.pbB[4])), ((k.pb[5], k.pbB[5]), (k.pb[6], k.pbB[6]))]
        sbanks = [(k.pb[0], k.pbB[0]), (k.pb[1], k.pbB[1]), (k.pb[2], k.pbB[2])]

        def d_scores(t):
            n, gi_, ngrp, nblk, c0, grp = items[t]
            pst, psB = sbanks[t % 3]
            fns = []
            m = 0
            while m < len(grp):
                m2 = m
                while m2 + 1 < len(grp) and grp[m2 + 1][0] == grp[m2][0] + 1:
                    m2 += 1
                bi0 = grp[m][0]
                nrun = m2 - m + 1
                fns.append(lambda e, m=m, bi0=bi0, nrun=nrun: e.matmul(
                    pst[:, m * 128:(m + nrun) * 128], lhsT=k.ident_b, rhs=tab[:, bi0:bi0 + nrun, :].rearrange("p b i -> p (b i)"), start=True, stop=False))
                for mm_ in range(m, m2 + 1):
                    _, g, nk = grp[mm_]
                    fns.append(lambda e, mm_=mm_, g=g, nk=nk: e.matmul(pst[:, mm_ * 128:(mm_ + 1) * 128], lhsT=kT[0:64, g, nk * 128:(nk + 1) * 128],
                                                                      rhs=qT[0:64, g, n * 128:(n + 1) * 128], start=False, stop=True))
                m = m2 + 1
            P.op("pe", fns, reads=[tabB, qkB, k.CB], writes=[psB])
            p_t, p_B = pm[t % 3], pmB[t % 3]
            wd = len(grp) * 128
            P.op("act", lambda e: e.activation(out=p_t[:, 0:wd], in_=pst[:, 0:wd], func=AF.Exp), reads=[psB], writes=[p_B])

        def d_pv(t):
            n, gi_, ngrp, nblk, c0, grp = items[t]
            (onum, onumB), (oden, odenB) = obanks[n % 2]
            p_t, p_B = pm[t % 3], pmB[t % 3]
            wd = len(grp) * 128
            fns = []
            for m, (bi, g, nk) in enumerate(grp):
                first = (c0 + m == 0)
                last = (c0 + m == nblk - 1)
                fns.append(lambda e, m=m, g=g, nk=nk, first=first, last=last: e.matmul(
                    onum[0:64, 0:128], lhsT=vv[:, g, nk, :], rhs=p_t[:, m * 128:(m + 1) * 128], start=first, stop=last))
            fns.append(lambda e: e.matmul(oden[0:64, 0:wd], lhsT=k.ones_b[:, 0:64], rhs=p_t[:, 0:wd], start=(gi_ == 0), stop=(gi_ == ngrp - 1)))
            P.op("pe", fns, reads=[p_B, vB, k.CB], writes=[onumB, odenB])
            if gi_ == ngrp - 1:
                W1 = min(nblk, 4) * 128
                if W1 > 128:
                    P.op("dve", lambda e: e.reduce_sum(out=rd[0:64, :], in_=oden[0:64, 0:W1].rearrange("p (m i) -> p i m", m=W1 // 128),
                                                        axis=mybir.AxisListType.X), reads=[odenB], writes=[rdB])
                    P.op("dve", lambda e: e.reciprocal(out=rd[0:64, :], in_=rd[0:64, :]), reads=[rdB], writes=[rdB])
                else:
                    P.op("dve", lambda e: e.reciprocal(out=rd[0:64, :], in_=oden[0:64, 0:128]), reads=[odenB], writes=[rdB])
                P.op("dve", lambda e, y_t=y_t: e.tensor_mul(out=y_t[0:64, n * 128:(n + 1) * 128], in0=onum[0:64, 0:128], in1=rd[0:64, :]),
                     reads=[onumB, rdB], writes=[y_B])

        G = len(items)
        for t in range(-1, G):
            if t + 1 < G:
                d_scores(t + 1)
            if t >= 0:
                d_pv(t)
        P.dma("sp", O["ybrT"][:, (j % 2) * 64:(j % 2) * 64 + 64, 4 + j // 2, :].rearrange("s p t -> p s t"), y_t[0:64, :].rearrange("p (s t) -> p s t", s=NS),
              k.sem("dy%d" % (j % 2)), reads=[y_B], writes=[k.DB["ybrT"]])
    P.barrier()


def stage_sb(k, l):
    P, A, I, O = k.P, k.A, k.I, k.O
    A.p = k.const_end
    msk = r3(A.bf16(4 * 512), 4)
    mskB = Buf("msk")
    P.dma("pool", msk, I["sbmask"].rearrange("d s t -> s d t"), k.sem("sbm"), writes=[mskB])
    qT = [A.bf16(S) for _ in range(4)]
    kT = [A.bf16(S) for _ in range(4)]
    vv = [r3(A.bf16(NT * 128), NT) for _ in range(4)]
    hB = [Buf("sbh%d" % i) for i in range(4)]
    e1 = [A.f32(512) for _ in range(2)]
    e1B = [Buf("e1%d" % i) for i in range(2)]
    sp = [A.bf16(512) for _ in range(3)]
    spB = [Buf("sp%d" % i) for i in range(3)]
    aa = [A.bf16(512) for _ in range(2)]
    aaB = [Buf("aa%d" % i) for i in range(2)]
    ssf = [A.f32(512) for _ in range(2)]
    ssfB = [Buf("ssf%d" % i) for i in range(2)]
    yo = [A.bf16(S) for _ in range(2)]
    yoB = [Buf("sby%d" % i) for i in range(2)]
    zb = [(k.pb[0], k.pbB[0]), (k.pb[1], k.pbB[1])]
    wb = [(k.pb[2], k.pbB[2]), (k.pb[3], k.pbB[3])]
    ob = [(k.pb[4], k.pbB[4]), (k.pb[5], k.pbB[5])]
    for h in range(4):
        P.dma("sp", qT[h], O["qkT"][1536 + h * 128:1536 + (h + 1) * 128, :], k.sem("sbq%d" % h), reads=[k.DB["qkT"]], writes=[hB[h]])
        P.dma("sp", kT[h], O["qkT"][2048 + h * 128:2048 + (h + 1) * 128, :], k.sem("sbq%d" % h), reads=[k.DB["qkT"]], writes=[hB[h]])
        P.dma("sp", vv[h].rearrange("p t e -> p (t e)"), O["vsb"][h], k.sem("sbq%d" % h), reads=[k.DB["vsb"]], writes=[hB[h]])
    pairs = []
    c = 0
    for h in range(4):
        for T in range(NS):
            nblk = 4 * T + 4
            for idx, b in enumerate(range(nblk - 1, -1, -1)):
                pairs.append((h, T, idx, b, nblk, b - 4 * T, c))
            c += 1
    N = len(pairs)

    def s1(i):
        h, T, idx, b, nblk, diag, c = pairs[i]
        zt, zB = zb[i % 2]
        kv = kT[h][:, b * 128:(b + 1) * 128]
        qv = qT[h][:, T * 512:(T + 1) * 512]
        P.op("pe", lambda e: e.matmul(zt[:, :], lhsT=kv, rhs=qv, start=True, stop=True), reads=[hB[h]], writes=[zB])

    def s2(i):
        h, T, idx, b, nblk, diag, c = pairs[i]
        zt, zB = zb[i % 2]
        e_t, e_B = e1[i % 2], e1B[i % 2]
        s_t, s_B = sp[i % 3], spB[i % 3]
        P.op("act", lambda e: e.activation(out=e_t, in_=zt[:, :], func=AF.Exp), reads=[zB], writes=[e_B])
        P.op("act", lambda e: e.activation(out=s_t, in_=e_t, func=AF.Ln, bias=1.0), reads=[e_B], writes=[s_B])
        if diag >= 0:
            P.op("dve", lambda e: e.tensor_mul(out=s_t, in0=s_t, in1=msk[:, diag, :]), reads=[s_B, mskB], writes=[s_B])

    def s3(i):
        h, T, idx, b, nblk, diag, c = pairs[i]
        wt, wB = wb[i % 2]
        s_t, s_B = sp[i % 3], spB[i % 3]
        kv = kT[h][:, b * 128:(b + 1) * 128]
        qv = qT[h][:, T * 512:(T + 1) * 512]
        fns = [lambda e: e.matmul(wt[:, :], lhsT=kv, rhs=qv, start=True, stop=False),
               lambda e: e.matmul(wt[:, :], lhsT=k.negU_b, rhs=s_t, start=False, stop=(idx == 0))]
        rds = [hB[h], s_B, k.CB]
        if idx > 0:
            sf_prev, sf_prevB = ssf[(idx - 1) % 2], ssfB[(idx - 1) % 2]
            fns.append(lambda e: e.matmul(wt[:, :], lhsT=k.negones_f, rhs=sf_prev, start=False, stop=True))
            rds.append(sf_prevB)
        P.op("pe", fns, reads=rds, writes=[wB])
        if idx < nblk - 1:
            if idx == 0:
                P.op("dve", lambda e: e.tensor_copy(out=ssf[0], in_=s_t), reads=[s_B], writes=[ssfB[0]])
            else:
                P.op("dve", lambda e: e.tensor_add(out=ssf[idx % 2], in0=ssf[(idx - 1) % 2], in1=s_t),
                     reads=[s_B, ssfB[(idx - 1) % 2]], writes=[ssfB[idx % 2]])

    def s4(i):
        h, T, idx, b, nblk, diag, c = pairs[i]
        wt, wB = wb[i % 2]
        a_t, a_B = aa[i % 2], aaB[i % 2]
        P.op("act", lambda e: e.activation(out=a_t, in_=wt[:, :], func=AF.Exp), reads=[wB], writes=[a_B])
        if diag >= 0:
            P.op("dve", lambda e: e.tensor_mul(out=a_t, in0=a_t, in1=msk[:, diag, :]), reads=[a_B, mskB], writes=[a_B])

    def s5(i):
        h, T, idx, b, nblk, diag, c = pairs[i]
        a_t, a_B = aa[i % 2], aaB[i % 2]
        ot, oB = ob[c % 2]
        P.op("pe", lambda e: e.matmul(ot[:, :], lhsT=vv[h][:, b, :], rhs=a_t, start=(idx == 0), stop=(idx == nblk - 1)),
             reads=[hB[h], a_B], writes=[oB])
        if idx == nblk - 1:
            y_t, y_B = yo[h % 2], yoB[h % 2]
            if T % 2 == 0:
                P.op("act", lambda e: e.activation(out=y_t[:, T * 512:(T + 1) * 512], in_=ot[:, :], func=AF.Copy), reads=[oB], writes=[y_B])
            else:
                P.op("dve", lambda e: e.tensor_copy(out=y_t[:, T * 512:(T + 1) * 512], in_=ot[:, :]), reads=[oB], writes=[y_B])
            if T == NS - 1:
                P.dma("sp", O["ybrT"][:, :, 6 + h, :].rearrange("s p t -> p s t"), y_t.rearrange("p (s t) -> p s t", s=NS), k.sem("sby%d" % (h % 2)), reads=[y_B], writes=[k.DB["ybrT"]])

    for t in range(-2, N):
        if 0 <= t + 2 < N:
            s1(t + 2)
            s2(t + 2)
        if 0 <= t + 1 < N:
            s3(t + 1)
            s4(t + 1)
        if 0 <= t < N:
            s5(t)
    P.barrier()


def stage_s5(k, l):
    P, A, I, O = k.P, k.A, k.I, k.O
    A.p = k.const_end
    PI = math.pi
    sm = lambda: A.f32(16)
    are, aim, ldt, dtv, rr, th, t0, t1, sn, cs = [sm() for _ in range(10)]
    lbr, lbi, nr, den, cre, cim, ncim, ere, eim, neim = [sm() for _ in range(10)]
    xre, xim = sm(), sm()
    t2x = sm()
    dcol = A.f32(4)
    prmB = Buf("s5prm")
    sp_ = k.sem("s5p")
    P.dma("sp", are, I["s5are"][l], sp_, writes=[prmB])
    P.dma("sp", aim, I["s5aim"][l], sp_, writes=[prmB])
    P.dma("sp", ldt, I["s5ldt"][l], sp_, writes=[prmB])
    P.dma("sp", dcol, I["s5_d"][l].rearrange("(kc p) -> p kc", p=128), sp_, writes=[prmB], allow_slow_non_contiguous=True)

    def pop(eng, f):
        P.op(eng, f, reads=[prmB, k.CB], writes=[prmB])

    MAGIC = 12582912.0
    INV2PI = 1.0 / (2 * PI)

    def sin_reduced(opf, x_ap, out_ap, tmp, tmp2, shift):
        if shift != 0.0:
            opf("dve", lambda e: e.tensor_scalar_add(out=tmp2, in0=x_ap, scalar1=shift))
            xs = tmp2
        else:
            xs = x_ap
        opf("dve", lambda e: e.tensor_scalar(out=tmp, in0=xs, scalar1=INV2PI, scalar2=MAGIC, op0=ALU.mult, op1=ALU.add))
        opf("dve", lambda e: e.tensor_scalar_add(out=tmp, in0=tmp, scalar1=-MAGIC))
        opf("dve", lambda e: e.scalar_tensor_tensor(out=tmp, in0=tmp, scalar=-2 * PI, in1=xs, op0=ALU.mult, op1=ALU.add))
        opf("dve", lambda e: e.tensor_scalar(out=tmp, in0=tmp, scalar1=PI, scalar2=-PI, op0=ALU.min, op1=ALU.max))
        opf("act", lambda e: e.activation(out=out_ap, in_=tmp, func=AF.Sin))

    def sincos(th_ap, s_out, c_out, tmp):
        sin_reduced(pop, th_ap, s_out, tmp, t2x, 0.0)
        sin_reduced(pop, th_ap, c_out, tmp, t2x, 0.5 * PI)

    pop("act", lambda e: e.activation(out=dtv, in_=ldt, func=AF.Exp))
    pop("dve", lambda e: e.tensor_mul(out=t0, in0=are, in1=dtv))
    pop("act", lambda e: e.activation(out=rr, in_=t0, func=AF.Exp))
    pop("dve", lambda e: e.tensor_mul(out=th, in0=aim, in1=dtv))
    sincos(th, sn, cs, t0)
    pop("dve", lambda e: e.tensor_mul(out=lbr, in0=rr, in1=cs))
    pop("dve", lambda e: e.tensor_mul(out=lbi, in0=rr, in1=sn))
    pop("dve", lambda e: e.tensor_scalar_add(out=nr, in0=lbr, scalar1=-1.0))
    pop("dve", lambda e: e.tensor_mul(out=den, in0=are, in1=are))
    pop("dve", lambda e: e.tensor_mul(out=t0, in0=aim, in1=aim))
    pop("dve", lambda e: e.tensor_add(out=den, in0=den, in1=t0))
    pop("dve", lambda e: e.reciprocal(out=den, in_=den))
    pop("dve", lambda e: e.tensor_mul(out=t0, in0=nr, in1=are))
    pop("dve", lambda e: e.tensor_mul(out=t1, in0=lbi, in1=aim))
    pop("dve", lambda e: e.tensor_add(out=t0, in0=t0, in1=t1))
    pop("dve", lambda e: e.tensor_mul(out=cre, in0=t0, in1=den))
    pop("dve", lambda e: e.tensor_mul(out=t0, in0=lbi, in1=are))
    pop("dve", lambda e: e.tensor_mul(out=t1, in0=nr, in1=aim))
    pop("dve", lambda e: e.tensor_sub(out=t0, in0=t0, in1=t1))
    pop("dve", lambda e: e.tensor_mul(out=cim, in0=t0, in1=den))
    pop("dve", lambda e: e.tensor_scalar_mul(out=ncim, in0=cim, scalar1=-1.0))
    pop("dve", lambda e: e.tensor_scalar_mul(out=t1, in0=th, scalar1=512.0))
    sincos(t1, eim, ere, t0)
    pop("dve", lambda e: e.tensor_scalar_mul(out=neim, in0=eim, scalar1=-1.0))
    pop("dve", lambda e: e.memset(xre, 0.0))
    pop("dve", lambda e: e.memset(xim, 0.0))
    tabB = Buf("s5tab")
    BUWr = r3(A.bf16(2048), 16)
    BUWi = r3(A.bf16(2048), 16)
    CWr = r3(A.bf16(2048), 16)
    CWi = r3(A.bf16(2048), 16)
    wB = Buf("s5w")
    P.dma("pool", CWr, I["s5cre"][l].rearrange("j s c -> s j c"), k.sem("s5c"), writes=[wB])
    P.dma("pool", CWi, I["s5cim"][l].rearrange("j s c -> s j c"), k.sem("s5c"), writes=[wB])
    P.op("dve", lambda e: e.tensor_scalar_mul(out=CWi, in0=CWi, scalar1=-1.0), reads=[wB], writes=[wB])
    nCWr = r3(A.bf16(2048), 16)
    P.op("dve", lambda e: e.tensor_scalar_mul(out=nCWr, in0=CWr, scalar1=-1.0), reads=[wB], writes=[wB])
    iota512 = A.f32(512)
    P.dma("sp", iota512, I["iota512"], k.sem("s5i"), writes=[wB])
    wglu = r3(A.bf16(4 * 1024), 4)
    P.dma("pool", wglu, I["s5_w_glu"][l].rearrange("(kc p) n -> p kc n", p=128), k.sem("s5g"), writes=[wB])
    bp = [(A.f32(128), A.f32(128)) for _ in range(2)]
    bpB = [Buf("bp%d" % i) for i in range(2)]
    bt = [A.f32(128) for _ in range(4)]
    btB = [Buf("bt%d" % i) for i in range(4)]
    rot = psum_rot(k, [0, 1, 2, 3])
    for j in range(16):
        (b_r, b_i), b_B = bp[j % 2], bpB[j % 2]
        P.dma("sp", b_r, I["s5bre"][l, j], k.sem("s5b%d" % (j % 2)), writes=[b_B])
        P.dma("sp", b_i, I["s5bim"][l, j], k.sem("s5b%d" % (j % 2)), writes=[b_B])
        tr, trB = bt[(2 * j) % 4], btB[(2 * j) % 4]
        ti, tiB = bt[(2 * j + 1) % 4], btB[(2 * j + 1) % 4]
        P.op("dve", lambda e, tr=tr, b_r=b_r, j=j: e.tensor_scalar(out=tr, in0=b_r, scalar1=cre[:, j:j + 1], scalar2=None, op0=ALU.mult), reads=[b_B, prmB], writes=[trB])
        P.op("dve", lambda e, tr=tr, b_i=b_i, j=j: e.scalar_tensor_tensor(out=tr, in0=b_i, scalar=ncim[:, j:j + 1], in1=tr, op0=ALU.mult, op1=ALU.add), reads=[b_B, prmB, trB], writes=[trB])
        P.op("dve", lambda e, ti=ti, b_i=b_i, j=j: e.tensor_scalar(out=ti, in0=b_i, scalar1=cre[:, j:j + 1], scalar2=None, op0=ALU.mult), reads=[b_B, prmB], writes=[tiB])
        P.op("dve", lambda e, ti=ti, b_r=b_r, j=j: e.scalar_tensor_tensor(out=ti, in0=b_r, scalar=cim[:, j:j + 1], in1=ti, op0=ALU.mult, op1=ALU.add), reads=[b_B, prmB, tiB], writes=[tiB])
        for (src, srcB, dstw) in ((tr, trB, BUWr), (ti, tiB, BUWi)):
            pbt, pbB = rot()
            P.op("pe", lambda e, pbt=pbt, src=src: e.transpose(out=pbt[:, 0:128], in_=src, identity=k.ident_f), reads=[srcB, k.CB], writes=[pbB])
            P.op("act", lambda e, pbt=pbt, dstw=dstw, j=j: e.activation(out=dstw[:, j, :], in_=pbt[:, 0:128], func=AF.Copy), reads=[pbB], writes=[wB])
    us5 = A.bf16(S)
    usB = Buf("us5")
    ygT = r3(A.bf16(4 * S), 4)
    ygB = Buf("ygT")
    cos5 = r3(A.f32(2048), 4)
    sin5 = r3(A.f32(2048), 4)
    R5 = r3(A.f32(2048), 4)
    p_main = A.p
    tin_reg = A.f32(8 * 512)
    tout_reg = A.f32(4 * 512)
    tin_bf = tin_reg.bitcast(BF16)
    tinb = [[tin_bf[:, (a * 4 + i) * 512:(a * 4 + i + 1) * 512] for i in range(4)] for a in range(2)]
    tinB = [[Buf("tin%d_%d" % (a, i)) for i in range(4)] for a in range(2)]
    tout_bf = tout_reg.bitcast(BF16)
    tout = [[tout_bf[:, (a * 4 + i) * 512:(a * 4 + i + 1) * 512] for i in range(4)] for a in range(2)]
    toutB = [[Buf("tout%d_%d" % (a, i)) for i in range(4)] for a in range(2)]
    ang = r3(tin_reg[:, 0:2048], 4)
    a2 = r3(tin_reg[:, 2048:4096], 4)
    a3 = r3(tout_reg[:, 0:2048], 4)
    bre = [A.f32(512) for _ in range(4)]
    bim = [A.f32(512) for _ in range(4)]
    gre = [A.f32(512) for _ in range(4)]
    gim = [A.f32(512) for _ in range(4)]
    breB = [Buf("bre%d" % i) for i in range(4)]
    bimB = [Buf("bim%d" % i) for i in range(4)]
    greB = [Buf("gre%d" % i) for i in range(4)]
    gimB = [Buf("gim%d" % i) for i in range(4)]
    XB = [Buf("X%d" % j) for j in range(16)]
    cx = [A.f32(2) for _ in range(4)]
    cxB = [Buf("cx%d" % i) for i in range(4)]
    ypre = A.f32(512)
    ypB = Buf("ypre")
    gu = A.f32(512)
    guB = Buf("gu")
    sg = A.f32(512)
    sgB = Buf("sg")
    brot = psum_rot(k, [0, 1, 2, 3])
    yrot = psum_rot(k, [6, 7])

    def tabop(eng, f):
        P.op(eng, f, reads=[tabB], writes=[tabB])
    for kc in range(4):
        P.barrier()
        for jj in range(4):
            j = 4 * kc + jj
            P.op("dve", lambda e, jj=jj, j=j: e.tensor_scalar(out=ang[:, jj, :], in0=iota512, scalar1=th[:, j:j + 1], scalar2=None, op0=ALU.mult),
                 reads=[prmB, wB], writes=[tabB])
            P.op("pool", lambda e, jj=jj, j=j: e.tensor_scalar(out=R5[:, jj, :], in0=iota512, scalar1=0.0, scalar2=rr[:, j:j + 1], op0=ALU.mult, op1=ALU.add),
                 reads=[prmB, wB], writes=[tabB])
        sin_reduced(tabop, ang, sin5, a2, a3, 0.0)
        sin_reduced(tabop, ang, cos5, a2, a3, 0.5 * PI)
        P.barrier()
        P.dma("sp", us5, O["qkT"][2560 + kc * 128:2560 + (kc + 1) * 128, :], k.sem("s5u"), reads=[k.DB["qkT"]], writes=[usB])
        for s in range(NS):
            usl = us5[:, s * 512:(s + 1) * 512]
            bu_banks = {}

            def e_bu(jj, usl=usl, kc=kc):
                j = 4 * kc + jj
                pr, prB = brot()
                pi_, piB = brot()
                bu_banks[jj] = (pr, prB, pi_, piB)
                P.op("pe", lambda e: e.matmul(pr[:, :], lhsT=BUWr[:, j, :], rhs=usl, start=True, stop=True), reads=[wB, usB], writes=[prB])
                P.op("pe", lambda e: e.matmul(pi_[:, :], lhsT=BUWi[:, j, :], rhs=usl, start=True, stop=True), reads=[wB, usB], writes=[piB])

            def e_prod(jj):
                pr, prB, pi_, piB = bu_banks[jj]
                cosB = cos5[:, jj, :]
                sinB = sin5[:, jj, :]
                tA, tB_, tC, tD = tinb[jj % 2]
                tAB, tBB, tCB, tDB = tinB[jj % 2]
                P.op("dve", lambda e: e.tensor_mul(out=tA, in0=pr[:, :], in1=cosB), reads=[prB, tabB], writes=[tAB])
                P.op("dve", lambda e: e.tensor_mul(out=tB_, in0=pi_[:, :], in1=sinB), reads=[piB, tabB], writes=[tBB])
                P.op("dve", lambda e: e.tensor_mul(out=tC, in0=pi_[:, :], in1=cosB), reads=[piB, tabB], writes=[tCB])
                P.op("dve", lambda e: e.tensor_mul(out=tD, in0=pr[:, :], in1=sinB), reads=[prB, tabB], writes=[tDB])

            def e_ident(jj):
                tA, tB_, tC, tD = tinb[jj % 2]
                tAB, tBB, tCB, tDB = tinB[jj % 2]
                bR, bRB = k.pb[4], k.pbB[4]
                bI, bIB = k.pb[5], k.pbB[5]
                P.op("pe", [lambda e: e.matmul(bR[:, :], lhsT=k.ident_b, rhs=tA, start=True, stop=False),
                            lambda e: e.matmul(bR[:, :], lhsT=k.ident_b, rhs=tB_, start=False, stop=True)], reads=[tAB, tBB, k.CB], writes=[bRB])
                P.op("pe", [lambda e: e.matmul(bI[:, :], lhsT=k.ident_b, rhs=tC, start=True, stop=False),
                            lambda e: e.matmul(bI[:, :], lhsT=k.negident_b, rhs=tD, start=False, stop=True)], reads=[tCB, tDB, k.CB], writes=[bIB])

            def e_scan(jj, kc=kc):
                j = 4 * kc + jj
                bR, bRB = k.pb[4], k.pbB[4]
                bI, bIB = k.pb[5], k.pbB[5]
                P.op("dve", lambda e: e.tensor_tensor_scan(out=gre[jj], data0=R5[:, jj, :], data1=bR[:, :], initial=xre[:, j:j + 1],
                                                          op0=ALU.mult, op1=ALU.add), reads=[bRB, XB[j], tabB], writes=[greB[jj]])
                P.op("dve", lambda e: e.tensor_tensor_scan(out=gim[jj], data0=R5[:, jj, :], data1=bI[:, :], initial=xim[:, j:j + 1],
                                                          op0=ALU.mult, op1=ALU.add), reads=[bIB, XB[j], tabB], writes=[gimB[jj]])

            e_bu(0); e_prod(0); e_bu(1); e_ident(0); e_prod(1); e_scan(0)
            e_bu(2); e_ident(1); e_prod(2); e_scan(1)
            e_bu(3); e_ident(2); e_prod(3); e_scan(2)
            e_ident(3); e_scan(3)
            for jj in range(4):
                j = 4 * kc + jj
                gl_r = gre[jj][:, 511:512]
                gl_i = gim[jj][:, 511:512]
                c_t, c_B = cx[jj], cxB[jj]
                P.op("dve", lambda e, gl_r=gl_r, j=j, c_t=c_t: e.tensor_tensor(out=c_t[:, 0:1], in0=gl_r, in1=ere[:, j:j + 1], op=ALU.mult), reads=[greB[jj], prmB], writes=[c_B])
                P.op("dve", lambda e, gl_i=gl_i, j=j, c_t=c_t: e.tensor_tensor(out=c_t[:, 1:2], in0=gl_i, in1=ere[:, j:j + 1], op=ALU.mult), reads=[gimB[jj], prmB], writes=[c_B])
                P.op("dve", lambda e, gl_i=gl_i, j=j, c_t=c_t: e.scalar_tensor_tensor(out=xre[:, j:j + 1], in0=gl_i, scalar=neim[:, j:j + 1], in1=c_t[:, 0:1], op0=ALU.mult, op1=ALU.add),
                     reads=[gimB[jj], c_B, prmB], writes=[XB[j]])
                P.op("dve", lambda e, gl_r=gl_r, j=j, c_t=c_t: e.scalar_tensor_tensor(out=xim[:, j:j + 1], in0=gl_r, scalar=eim[:, j:j + 1], in1=c_t[:, 1:2], op0=ALU.mult, op1=ALU.add),
                     reads=[greB[jj], c_B, prmB], writes=[XB[j]])
            yt, yB = yrot()
            fns = []
            yreads = [wB]
            for jj in range(4):
                j = 4 * kc + jj
                cosB = cos5[:, jj, :]
                sinB = sin5[:, jj, :]
                tE, tF, tG, tH = tout[jj % 2]
                tEB, tFB, tGB, tHB = toutB[jj % 2]
                P.op("pool", lambda e, jj=jj, cosB=cosB, tE=tE: e.tensor_mul(out=tE, in0=gre[jj], in1=cosB), reads=[greB[jj], tabB], writes=[tEB])
                P.op("pool", lambda e, jj=jj, sinB=sinB, tF=tF: e.tensor_mul(out=tF, in0=gim[jj], in1=sinB), reads=[gimB[jj], tabB], writes=[tFB])
                P.op("pool", lambda e, jj=jj, cosB=cosB, tG=tG: e.tensor_mul(out=tG, in0=gim[jj], in1=cosB), reads=[gimB[jj], tabB], writes=[tGB])
                P.op("pool", lambda e, jj=jj, sinB=sinB, tH=tH: e.tensor_mul(out=tH, in0=gre[jj], in1=sinB), reads=[greB[jj], tabB], writes=[tHB])
                fns = [lambda e, yt=yt, j=j, tE=tE, jj=jj: e.matmul(yt[:, :], lhsT=CWr[:, j, :], rhs=tE, start=(jj == 0), stop=False),
                       lambda e, yt=yt, j=j, tF=tF: e.matmul(yt[:, :], lhsT=nCWr[:, j, :], rhs=tF, start=False, stop=False),
                       lambda e, yt=yt, j=j, tG=tG: e.matmul(yt[:, :], lhsT=CWi[:, j, :], rhs=tG, start=False, stop=False),
                       lambda e, yt=yt, j=j, tH=tH, jj=jj: e.matmul(yt[:, :], lhsT=CWi[:, j, :], rhs=tH, start=False, stop=(jj == 3))]
                P.op("pe", fns, reads=[wB, tEB, tFB, tGB, tHB], writes=[yB])
            P.op("dve", lambda e, yt=yt, usl=usl, kc=kc: e.scalar_tensor_tensor(out=ypre, in0=usl, scalar=dcol[:, kc:kc + 1], in1=yt[:, :], op0=ALU.mult, op1=ALU.add),
                 reads=[yB, usB, prmB], writes=[ypB])
            P.op("pool", lambda e: e.tensor_mul(out=gu, in0=ypre, in1=ypre), reads=[ypB], writes=[guB])
            P.op("pool", lambda e: e.tensor_scalar(out=gu, in0=gu, scalar1=0.044715, scalar2=1.0, op0=ALU.mult, op1=ALU.add), reads=[guB], writes=[guB])
            P.op("pool", lambda e: e.tensor_mul(out=gu, in0=gu, in1=ypre), reads=[guB, ypB], writes=[guB])
            P.op("act", lambda e: e.activation(out=sg, in_=gu, func=AF.Sigmoid, scale=1.5957691216057308), reads=[guB], writes=[sgB])
            P.op("dve", lambda e, kc=kc, s=s: e.tensor_mul(out=ygT[:, kc, s * 512:(s + 1) * 512], in0=sg, in1=ypre), reads=[sgB, ypB], writes=[ygB])
    P.barrier()
    A.p = p_main
    yo = [A.bf16(S) for _ in range(2)]
    yoB = [Buf("s5yo%d" % i) for i in range(2)]
    grot = psum_rot(k, [0, 1, 2, 3])
    for oc in range(4):
        y_t, y_B = yo[oc % 2], yoB[oc % 2]
        for s in range(NS):
            vt_, vB_ = grot()
            gt_, gB_ = grot()
            P.op("pe", [lambda e, vt_=vt_, kc=kc, oc=oc, s=s: e.matmul(vt_[:, :], lhsT=wglu[:, kc, oc * 128:(oc + 1) * 128], rhs=ygT[:, kc, s * 512:(s + 1) * 512],
                                                                     start=(kc == 0), stop=(kc == 3)) for kc in range(4)], reads=[wB, ygB], writes=[vB_])
            P.op("pe", [lambda e, gt_=gt_, kc=kc, oc=oc, s=s: e.matmul(gt_[:, :], lhsT=wglu[:, kc, 512 + oc * 128:512 + (oc + 1) * 128], rhs=ygT[:, kc, s * 512:(s + 1) * 512],
                                                                     start=(kc == 0), stop=(kc == 3)) for kc in range(4)], reads=[wB, ygB], writes=[gB_])
            P.op("act", lambda e, gt_=gt_: e.activation(out=sg, in_=gt_[:, :], func=AF.Sigmoid), reads=[gB_], writes=[sgB])
            P.op("dve", lambda e, vt_=vt_, y_t=y_t, s=s: e.tensor_mul(out=y_t[:, s * 512:(s + 1) * 512], in0=vt_[:, :], in1=sg), reads=[vB_, sgB], writes=[y_B])
        P.dma("sp", O["ybrT"][:, :, 10 + oc, :].rearrange("s p t -> p s t"), y_t.rearrange("p (s t) -> p s t", s=NS), k.sem("s5y%d" % (oc % 2)), reads=[y_B], writes=[k.DB["ybrT"]])
    P.barrier()


def stage_merge(k, l, cur, nxt):
    P, A, I, O = k.P, k.A, k.I, k.O
    A.p = k.const_end
    wbr = r3(A.bf16(14 * 1024), 14)
    wout = r3(A.bf16(8 * 1024), 8)
    wB = Buf("mw")
    P.dma("pool", wbr, I["w_branch"][l].rearrange("(kc p) n -> p kc n", p=128), k.sem("mw"), writes=[wB])
    P.dma("pool", wout, I["w_out"][l].rearrange("(kc p) n -> p kc n", p=128), k.sem("mw"), writes=[wB])
    ybr = [r3(A.bf16(14 * 512), 14) for _ in range(2)]
    ybrB = [Buf("ybr%d" % i) for i in range(2)]
    gt = r3(A.bf16(32 * 512), 32)
    gtB = Buf("gt")
    xs = [r3(A.f32(4096), 8) for _ in range(3)]
    xsB = [Buf("mxs%d" % i) for i in range(3)]
    mTs = [r3(A.bf16(8 * 512), 8) for _ in range(2)]
    mTBs = [Buf("mT%d" % i) for i in range(2)]
    mm = [A.f32(512) for _ in range(4)]
    mmB = [Buf("mm%d" % i) for i in range(4)]
    bbanks = [(k.pb[i], k.pbB[i]) for i in range(4)]
    orot = psum_rot(k, [4, 5, 6, 7])
    kcs = ((0, 4), (4, 6), (6, 10), (10, 14))
    ysrc = O["ybrT"]
    gsrc = O["gate"]
    xsrc = O[cur].rearrange("(kc p) t -> p kc t", p=128)
    xdst = O[nxt].rearrange("(kc p) t -> p kc t", p=128)
    for s in range(NS + 1):
        if s < NS:
            sl = slice(s * 512, (s + 1) * 512)
            y_t, y_B = ybr[s % 2], ybrB[s % 2]
            x_t, x_B = xs[s % 3], xsB[s % 3]
            mT, mTB = mTs[s % 2], mTBs[s % 2]
            P.dma("sp", y_t.rearrange("p a b -> p (a b)"), ysrc[s].rearrange("p a b -> p (a b)"), k.sem("my%d" % (s % 2)), reads=[k.DB["ybrT"]], writes=[y_B])
            P.dma("sp", gt.rearrange("p a b -> p (a b)"), gsrc[s].rearrange("p a b -> p (a b)"), k.sem("mg"), reads=[k.DB["gate"]], writes=[gtB])
            P.dma("sp", x_t, xsrc[:, :, sl], k.sem("mx%d" % (s % 3)), reads=[k.DB[cur]], writes=[x_B])
        if s >= 1:
            sp_ = s - 1
            px_t, px_B = xs[sp_ % 3], xsB[sp_ % 3]
            pmT, pmTB = mTs[sp_ % 2], mTBs[sp_ % 2]
        for co in range(8):
            if s < NS:
                for b in range(4):
                    pbt, pbB = bbanks[b]
                    k0, k1 = kcs[b]
                    P.op("pe", [lambda e, pbt=pbt, kc=kc, co=co, y_t=y_t, k0=k0, k1=k1: e.matmul(pbt[:, :], lhsT=wbr[:, kc, co * 128:(co + 1) * 128], rhs=y_t[:, kc, :],
                                                                                                start=(kc == k0), stop=(kc == k1 - 1)) for kc in range(k0, k1)],
                         reads=[wB, y_B], writes=[pbB])
            if s >= 1:
                dc = co
                obt, obB = orot()
                P.op("pe", [lambda e, obt=obt, c2=c2, dc=dc, pmT=pmT: e.matmul(obt[:, :], lhsT=wout[:, c2, dc * 128:(dc + 1) * 128], rhs=pmT[:, c2, :], start=(c2 == 0), stop=(c2 == 7))
                            for c2 in range(8)], reads=[wB, pmTB], writes=[obB])
            if s < NS:
                for b in range(4):
                    pbt, pbB = bbanks[b]
                    P.op("dve", lambda e, pbt=pbt, b=b, co=co: e.tensor_mul(out=mm[b], in0=pbt[:, :], in1=gt[:, b * 8 + co, :]), reads=[pbB, gtB], writes=[mmB[b]])
                P.op("dve", lambda e: e.tensor_add(out=mm[0], in0=mm[0], in1=mm[1]), reads=[mmB[1]], writes=[mmB[0]])
                P.op("dve", lambda e: e.tensor_add(out=mm[2], in0=mm[2], in1=mm[3]), reads=[mmB[3]], writes=[mmB[2]])
                P.op("dve", lambda e, co=co, mT=mT: e.tensor_add(out=mT[:, co, :], in0=mm[0], in1=mm[2]), reads=[mmB[0], mmB[2]], writes=[mTB])
            if s >= 1:
                P.op("dve", lambda e, obt=obt, dc=dc, px_t=px_t: e.tensor_add(out=px_t[:, dc, :], in0=obt[:, :], in1=px_t[:, dc, :]), reads=[obB], writes=[px_B])
        if s >= 1:
            psl = slice(sp_ * 512, (sp_ + 1) * 512)
            P.dma("sp", xdst[:, :, psl], px_t, k.sem("mo%d" % (sp_ % 3)), reads=[px_B], writes=[k.DB[nxt]])
    P.barrier()


def stage_ffn(k, l, cur, nxt):
    P, A, I, O = k.P, k.A, k.I, k.O
    A.p = k.const_end
    hnT = r3(A.bf16(8 * S), 8)
    hnB = [Buf("hn%d" % s) for s in range(NS)]
    p1 = A.p
    xs = [r3(A.f32(4096), 8) for _ in range(2)]
    xsB = [Buf("fxs%d" % i) for i in range(2)]
    sq = r3(A.bf16(4096), 8)
    sqB = Buf("fsq")
    rt = A.f32(512)
    rtB = Buf("frt")
    src = O[cur].rearrange("(kc p) t -> p kc t", p=128)
    for s in range(NS):
        P.dma("sp", xs[s % 2], src[:, :, s * 512:(s + 1) * 512], k.sem("fx%d" % (s % 2)), reads=[k.DB[cur]], writes=[xsB[s % 2]])
        rmsnorm_supertile(k, xs[s % 2], xsB[s % 2], 4 + l, hnT[:, :, s * 512:(s + 1) * 512], hnB[s], sq, sqB, rt, rtB, (k.pb[7], k.pbB[7]))
    P.barrier()
    A.p = p1
    wg = [r3(A.bf16(8 * 512), 8) for _ in range(2)]
    wu = [r3(A.bf16(8 * 512), 8) for _ in range(2)]
    wgB = [Buf("wg%d" % i) for i in range(2)]
    hrow = [A.bf16(S) for _ in range(2)]
    hrowB = [Buf("hrow%d" % i) for i in range(2)]
    sg = [A.f32(512) for _ in range(2)]
    sgB = [Buf("fsg%d" % i) for i in range(2)]
    wup = I["w_up"][l].rearrange("(kc p) n -> p kc n", p=128)
    rot = psum_rot(k, [0, 1, 2, 3, 4, 5])
    si = 0
    hi = 0
    ei = 0
    for c0 in range(0, FH, 512):
        cw = min(512, FH - c0)
        w_g, w_u, w_B = wg[si % 2], wu[si % 2], wgB[si % 2]
        P.dma("pool", w_g[:, :, 0:cw], wup[:, :, c0:c0 + cw], k.sem("fw%d" % (si % 2)), writes=[w_B])
        P.dma("pool", w_u[:, :, 0:cw], wup[:, :, FH + c0:FH + c0 + cw], k.sem("fw%d" % (si % 2)), writes=[w_B])
        si += 1
        for cc in range(cw // 128):
            h_t, h_B = hrow[hi % 2], hrowB[hi % 2]
            for s in range(NS):
                gt_, gB_ = rot()
                ut_, uB_ = rot()
                P.op("pe", [lambda e, gt_=gt_, kc=kc, w_g=w_g, cc=cc, s=s: e.matmul(gt_[:, :], lhsT=w_g[:, kc, cc * 128:(cc + 1) * 128], rhs=hnT[:, kc, s * 512:(s + 1) * 512],
                                                                                   start=(kc == 0), stop=(kc == 7)) for kc in range(8)], reads=[w_B, hnB[s]], writes=[gB_])
                P.op("pe", [lambda e, ut_=ut_, kc=kc, w_u=w_u, cc=cc, s=s: e.matmul(ut_[:, :], lhsT=w_u[:, kc, cc * 128:(cc + 1) * 128], rhs=hnT[:, kc, s * 512:(s + 1) * 512],
                                                                                   start=(kc == 0), stop=(kc == 7)) for kc in range(8)], reads=[w_B, hnB[s]], writes=[uB_])
                s_t, s_B = sg[ei % 2], sgB[ei % 2]
                ei += 1
                P.op("act", lambda e, gt_=gt_, s_t=s_t: e.activation(out=s_t, in_=gt_[:, :], func=AF.Silu), reads=[gB_], writes=[s_B])
                P.op("dve", lambda e, ut_=ut_, s_t=s_t, h_t=h_t, s=s: e.tensor_mul(out=h_t[:, s * 512:(s + 1) * 512], in0=ut_[:, :], in1=s_t), reads=[uB_, s_B], writes=[h_B])
            r0 = c0 + cc * 128
            P.dma("sp", O["hT"][r0:r0 + 128, :], h_t, k.sem("fh%d" % (hi % 2)), reads=[h_B], writes=[k.DB["hT"]])
            hi += 1
    P.barrier()
    A.p = k.const_end
    wdn = r3(A.bf16(22 * 1024), 22)
    wdB = Buf("wdn")
    P.dma("pool", wdn, I["w_down"][l].rearrange("(kc p) n -> p kc n", p=128), k.sem("fwd"), writes=[wdB])
    hs = [r3(A.bf16(22 * 512), 22) for _ in range(2)]
    hsB = [Buf("hs%d" % i) for i in range(2)]
    xs = [r3(A.f32(4096), 8) for _ in range(2)]
    xsB = [Buf("dxs%d" % i) for i in range(2)]
    hsrc = O["hT"].rearrange("(kc p) t -> p kc t", p=128)
    xdst = O[nxt].rearrange("(kc p) t -> p kc t", p=128)
    rot = psum_rot(k, [0, 1, 2, 3])
    for s in range(NS):
        sl = slice(s * 512, (s + 1) * 512)
        h_t, h_B = hs[s % 2], hsB[s % 2]
        x_t, x_B = xs[s % 2], xsB[s % 2]
        P.dma("sp", h_t, hsrc[:, :, sl], k.sem("dh%d" % (s % 2)), reads=[k.DB["hT"]], writes=[h_B])
        P.dma("sp", x_t, src[:, :, sl], k.sem("dx%d" % (s % 2)), reads=[k.DB[cur]], writes=[x_B])
        for dc in range(8):
            pbt, pbB = rot()
            P.op("pe", [lambda e, pbt=pbt, kc=kc, dc=dc, h_t=h_t: e.matmul(pbt[:, :], lhsT=wdn[:, kc, dc * 128:(dc + 1) * 128], rhs=h_t[:, kc, :], start=(kc == 0), stop=(kc == 21))
                        for kc in range(22)], reads=[wdB, h_B], writes=[pbB])
            P.op("dve", lambda e, pbt=pbt, dc=dc, x_t=x_t: e.tensor_add(out=x_t[:, dc, :], in0=pbt[:, :], in1=x_t[:, dc, :]), reads=[pbB], writes=[x_B])
        P.dma("sp", xdst[:, :, sl], x_t, k.sem("do%d" % (s % 2)), reads=[x_B], writes=[k.DB[nxt]])
    P.barrier()


def stage_final(k, cur):
    P, A, I, O = k.P, k.A, k.I, k.O
    A.p = k.const_end
    xs = [r3(A.f32(4096), 8) for _ in range(2)]
    xsB = [Buf("zxs%d" % i) for i in range(2)]
    xn = r3(A.f32(4096), 8)
    xnB = Buf("zxn")
    sq = r3(A.bf16(4096), 8)
    sqB = Buf("zsq")
    rt = A.f32(512)
    rtB = Buf("zrt")
    ot = [A.f32(1024) for _ in range(2)]
    otB = [Buf("zot%d" % i) for i in range(2)]
    src = O[cur].rearrange("(kc p) t -> p kc t", p=128)
    rot = psum_rot(k, [0, 1, 2, 3])
    for s in range(NS):
        P.dma("sp", xs[s % 2], src[:, :, s * 512:(s + 1) * 512], k.sem("zx%d" % (s % 2)), reads=[k.DB[cur]], writes=[xsB[s % 2]])
        rmsnorm_supertile(k, xs[s % 2], xsB[s % 2], 8, xn, xnB, sq, sqB, rt, rtB, (k.pb[7], k.pbB[7]), out_engines=("dve",))
        for tt in range(4):
            t = s * 4 + tt
            o_t, o_B = ot[t % 2], otB[t % 2]
            for half in range(2):
                pbt, pbB = rot()
                P.op("pe", [lambda e, pbt=pbt, q=q, half=half, tt=tt: e.transpose(out=pbt[:, q * 128:(q + 1) * 128], in_=xn[:, half * 4 + q, tt * 128:(tt + 1) * 128], identity=k.ident_f)
                            for q in range(4)], reads=[xnB, k.CB], writes=[pbB])
                if half == 0:
                    P.op("dve", lambda e, pbt=pbt, o_t=o_t: e.tensor_copy(out=o_t[:, 0:512], in_=pbt[:, :]), reads=[pbB], writes=[o_B])
                else:
                    P.op("act", lambda e, pbt=pbt, o_t=o_t: e.activation(out=o_t[:, 512:1024], in_=pbt[:, :], func=AF.Copy), reads=[pbB], writes=[o_B])
            P.dma("sp", O["out"][t * 128:(t + 1) * 128, :], o_t, k.sem("zo%d" % (t % 2)), reads=[o_B])
    P.barrier()


_PROG_CACHE = {}


def make_in_maps(inputs, ncores=8):
    consts = build_consts()
    tabs = build_dil_tables(np.asarray(inputs["rel_bias"], np.float32))
    s5l = s5_layouts({kk: np.asarray(inputs[kk], np.float32) for kk in
                      ("s5_a_re", "s5_a_im", "s5_log_dt", "s5_b_re", "s5_b_im", "s5_c_re", "s5_c_im")})
    shared = {}
    for nm in ("attn_norm_g", "w_in", "pool_w", "pool_scale", "s5_d", "s5_w_glu", "w_branch", "w_gate", "w_out",
               "ffn_norm_g", "w_up", "w_down", "final_norm_g"):
        shared[nm] = np.ascontiguousarray(np.asarray(inputs[nm], np.float32))
    shared["dil_tab"] = tabs
    shared.update(consts)
    shared.update(s5l)
    x = np.asarray(inputs["x"], np.float32)
    maps = []
    for c in range(ncores):
        m = dict(shared)
        m["x"] = np.ascontiguousarray(x[c])
        maps.append(m)
    return maps


def kernel(**inputs):
    if "full" not in _PROG_CACHE:
        _PROG_CACHE["full"] = build_program(NL)
    nc = _PROG_CACHE["full"]
    maps = make_in_maps(inputs, 8)
    res = run_bass_kernel_spmd(nc, maps, core_ids=list(range(8)))
    out = np.stack([np.asarray(r["out"], np.float32) for r in res.results], 0)
    return out
```

```python
import math
import numpy as np
from contextlib import ExitStack
import concourse.bass as bass
import concourse.mybir as mybir
from concourse.bass_utils import run_bass_kernel_spmd

F32 = mybir.dt.float32
BF16 = mybir.dt.bfloat16
AF = mybir.ActivationFunctionType
ALU = mybir.AluOpType

S = 4096
D = 1024
NL = 4
NT = 32
NS = 8
FH = 2816
EPS = 1e-6
NEG = -30000.0
DILS = (1, 4, 16)


class Buf:
    __slots__ = ("name", "w", "rs")

    def __init__(self, name=""):
        self.name = name
        self.w = None
        self.rs = []


class Prog:
    CE = ("pe", "act", "dve", "pool")

    def __init__(self, nc, es):
        self.nc = nc
        self.es = es
        self.ops = {e: [] for e in ("pe", "act", "dve", "pool", "sp")}
        self.esem = {e: es.enter_context(nc.semaphore("es_" + e)) for e in self.CE}
        self.ecnt = {e: 0 for e in self.CE}
        self.dsems = []

    def dma_sem(self, name):
        s = self.es.enter_context(self.nc.semaphore("ds_%s_%d" % (name, len(self.dsems))))
        d = {"s": s, "n": 0}
        self.dsems.append(d)
        return d

    def _waits(self, reads, writes, extra):
        toks = list(extra)
        for b in reads:
            if b.w is not None:
                toks.append(b.w)
        for b in writes:
            if b.w is not None:
                toks.append(b.w)
            toks.extend(b.rs)
        return toks

    def _commit(self, tok, reads, writes):
        for b in reads:
            b.rs.append(tok)
            if len(b.rs) > 64:
                b.rs = b.rs[-64:]
        for b in writes:
            b.w = tok
            b.rs = []

    def op(self, eng, fns, reads=(), writes=(), waits=()):
        if not isinstance(fns, (list, tuple)):
            fns = [fns]
        toks = self._waits(reads, writes, waits)
        self.ecnt[eng] += 1
        tok = (self.esem[eng], self.ecnt[eng])
        self.ops[eng].append((list(fns), toks, (self.esem[eng], 1)))
        self._commit(tok, reads, writes)
        return tok

    def dma(self, eng, out, in_, sem, reads=(), writes=(), waits=(), **kw):
        toks = self._waits(reads, writes, waits)
        sem["n"] += 16
        tok = (sem["s"], sem["n"])
        self.ops[eng].append(([lambda e: e.dma_start(out=out, in_=in_, **kw)], toks, (sem["s"], 16)))
        self._commit(tok, reads, writes)
        return tok

    def all_tokens(self):
        toks = [(self.esem[e], self.ecnt[e]) for e in self.CE if self.ecnt[e] > 0]
        toks += [(d["s"], d["n"]) for d in self.dsems if d["n"] > 0]
        return toks

    def barrier(self):
        toks = self.all_tokens()
        for e in self.ops:
            self.ops[e].append(([], list(toks), None))

    def _replay(self, eng, e):
        waited = {}
        for fns, toks, inc in self.ops[eng]:
            need = {}
            for s, v in toks:
                k = id(s)
                if waited.get(k, 0) >= v:
                    continue
                if k not in need or need[k][1] < v:
                    need[k] = (s, v)
            for k, (s, v) in need.items():
                e.wait_ge(s, v)
                waited[k] = v
            ins = None
            for f in fns:
                ins = f(e)
            if inc is not None and ins is not None:
                ins.then_inc(inc[0], inc[1])

    def build(self):
        with self.nc.Block() as block:
            @block.tensor
            def _(e):
                self._replay("pe", e)

            @block.scalar
            def _(e):
                self._replay("act", e)

            @block.vector
            def _(e):
                self._replay("dve", e)

            @block.gpsimd
            def _(e):
                self._replay("pool", e)

            @block.sync
            def _(e):
                self._replay("sp", e)


class Arena:
    def __init__(self, t, ncols):
        self.t = t
        self.n = ncols
        self.p = 0

    def reset(self):
        self.p = 0

    def f32(self, cols, shape=None):
        a = self.p
        self.p += cols
        assert self.p <= self.n, "arena overflow %d > %d" % (self.p, self.n)
        v = self.t[:, a:a + cols]
        return v

    def bf16(self, cols):
        c32 = (cols + 1) // 2
        a = self.p
        self.p += c32
        assert self.p <= self.n, "arena overflow %d > %d" % (self.p, self.n)
        return self.t[:, a:a + c32].bitcast(BF16)[:, 0:cols]


def r3(ap, a):
    return ap.rearrange("p (a b) -> p a b", a=a)


def _t5_bucket_np(dist):
    exact = 16
    df = np.maximum(dist, 1).astype(np.float32)
    large = exact + (np.log(df / np.float32(exact)) / np.float32(math.log(2048 / exact))
                     * np.float32(32 - exact)).astype(np.int32)
    large = np.minimum(large, 31)
    return np.where(dist < exact, dist, large)


def dil_block_list():
    bl = []
    for g, d in enumerate(DILS):
        for delta in range(d + 1):
            bl.append((g, delta))
    return bl


def build_dil_tables(rel_bias):
    bl = dil_block_list()
    c = np.arange(128)[:, None]
    i = np.arange(128)[None, :]
    out = np.empty((4, len(bl), 128, 128), np.float32)
    for j in range(4):
        for bi, (g, delta) in enumerate(bl):
            d = DILS[g]
            dist = 128 * delta + i - c
            valid = (dist >= 0) & (dist % d == 0) & (dist <= 128 * d)
            bucket = _t5_bucket_np(np.clip(dist, 0, 128 * d))
            vals = rel_bias[bucket, 4 * g + j]
            out[j, bi] = np.where(valid, vals, np.float32(NEG))
    return out


def build_consts():
    c = {}
    c["ident"] = np.eye(128, dtype=np.float32)
    j = np.arange(128)[:, None]
    s = np.arange(128)[None, :]
    c["negU"] = np.where(j >= s, -1.0, 0.0).astype(np.float32)
    t = np.arange(512)[None, :]
    c["sbmask"] = np.stack([((128 * d + j) < t) for d in range(4)], 0).astype(np.float32)
    c["iota"] = np.broadcast_to(np.arange(128, dtype=np.float32)[None, :], (128, 128)).copy()
    c["iota512"] = np.broadcast_to(np.arange(512, dtype=np.float32)[None, :], (128, 512)).copy()
    inv = np.zeros((4, 16), np.float32)
    for gi, w in enumerate((2, 4, 8, 16)):
        tt = np.arange(16)
        inv[gi] = 1.0 / np.minimum(tt + 1, w)
    c["poolinv"] = np.broadcast_to(inv.reshape(1, 64), (128, 64)).copy()
    return c


def s5_layouts(inp):
    o = {}
    def st(a):
        return np.ascontiguousarray(a.reshape(4, 16, 2, 64).transpose(0, 2, 3, 1).reshape(4, 128, 16))
    o["s5are"] = st(inp["s5_a_re"])
    o["s5aim"] = st(inp["s5_a_im"])
    ldt = np.broadcast_to(inp["s5_log_dt"][:, :, None], (4, 32, 64))
    o["s5ldt"] = st(np.ascontiguousarray(ldt))
    def bpad(b):
        out = np.zeros((4, 16, 128, 128), np.float32)
        for g in range(32):
            jj, g2 = g // 2, g % 2
            gc = g % 8
            out[:, jj, g2 * 64:(g2 + 1) * 64, gc * 16:(gc + 1) * 16] = b[:, g]
        return out
    o["s5bre"] = bpad(inp["s5_b_re"])
    o["s5bim"] = bpad(inp["s5_b_im"])
    def cpad(c):
        out = np.zeros((4, 16, 128, 128), np.float32)
        for g in range(32):
            jj, g2 = g // 2, g % 2
            gc = g % 8
            out[:, jj, g2 * 64:(g2 + 1) * 64, gc * 16:(gc + 1) * 16] = c[:, g].transpose(0, 2, 1)
        return out
    o["s5cre"] = cpad(inp["s5_c_re"])
    o["s5cim"] = cpad(inp["s5_c_im"])
    return o


class K:
    pass


def build_program(nlayers=NL, debug=False, stages=None):
    nc = bass.Bass("TRN2", target_bir_lowering=False)
    k = K()
    k.nc = nc
    k.debug = debug
    dt_in = lambda name, shape: nc.dram_tensor(name, list(shape), F32, kind="ExternalInput").ap()
    I = {}
    I["x"] = dt_in("x", [S, D])
    I["attn_norm_g"] = dt_in("attn_norm_g", [NL, D])
    I["w_in"] = dt_in("w_in", [NL, D, 4864])
    I["pool_w"] = dt_in("pool_w", [NL, 4, 128, 128])
    I["pool_scale"] = dt_in("pool_scale", [NL, 512])
    I["s5_d"] = dt_in("s5_d", [NL, 512])
    I["s5_w_glu"] = dt_in("s5_w_glu", [NL, 512, 1024])
    I["w_branch"] = dt_in("w_branch", [NL, 1792, D])
    I["w_gate"] = dt_in("w_gate", [NL, 4, D, D])
    I["w_out"] = dt_in("w_out", [NL, D, D])
    I["ffn_norm_g"] = dt_in("ffn_norm_g", [NL, D])
    I["w_up"] = dt_in("w_up", [NL, D, 2 * FH])
    I["w_down"] = dt_in("w_down", [NL, FH, D])
    I["final_norm_g"] = dt_in("final_norm_g", [D])
    I["dil_tab"] = dt_in("dil_tab", [4, 24, 128, 128])
    I["ident"] = dt_in("ident", [128, 128])
    I["negU"] = dt_in("negU", [128, 128])
    I["sbmask"] = dt_in("sbmask", [4, 128, 512])
    I["iota"] = dt_in("iota", [128, 128])
    I["iota512"] = dt_in("iota512", [128, 512])
    I["poolinv"] = dt_in("poolinv", [128, 64])
    for nm in ("s5are", "s5aim", "s5ldt"):
        I[nm] = dt_in(nm, [NL, 128, 16])
    for nm in ("s5bre", "s5bim", "s5cre", "s5cim"):
        I[nm] = dt_in(nm, [NL, 16, 128, 128])
    k.I = I
    kind_dbg = "ExternalOutput" if debug else "Internal"
    O = {}
    O["out"] = nc.dram_tensor("out", [S, D], F32, kind="ExternalOutput").ap()
    scr = lambda name, shape, dt: nc.dram_tensor(name, list(shape), dt, kind=kind_dbg).ap()
    O["xTa"] = scr("xTa", [D, S], F32)
    O["xTb"] = scr("xTb", [D, S], F32)
    O["upool"] = scr("upool", [512, S], F32)
    O["qkT"] = scr("qkT", [3072, S], BF16)
    O["gate"] = scr("gate", [NS, 128, 32, 512], BF16)
    O["vdil"] = scr("vdil", [12, 128, NT * 64], BF16)
    O["vsb"] = scr("vsb", [4, 128, NT * 128], BF16)
    O["ybrT"] = scr("ybrT", [NS, 128, 14, 512], BF16)
    O["hT"] = scr("hT", [FH, S], BF16)
    k.O = O

    with ExitStack() as es:
        P = Prog(nc, es)
        k.P = P
        arena_cols = 50 * 1024
        arena_t = es.enter_context(nc.sbuf_tensor("arena", [128, arena_cols], F32))
        k.A = Arena(arena_t, arena_cols)
        k.pb = [es.enter_context(nc.psum_tensor("pb%d" % i, [128, 512], F32)) for i in range(8)]
        k.pbB = [Buf("pb%d" % i) for i in range(8)]
        k.DB = {n: Buf(n) for n in O}
        k.sems = {}

        def sem(name):
            if name not in k.sems:
                k.sems[name] = P.dma_sem(name)
            return k.sems[name]
        k.sem = sem

        stage_consts(k)
        cur, nxt = "xTa", "xTb"
        if stages is None or "load" in stages:
            stage_load_x(k, cur)
        for l in range(nlayers):
            if stages is None or "A" in stages:
                stage_A(k, l, cur)
            if stages is None or "pool" in stages:
                stage_pool(k, l)
            if stages is None or "dil" in stages:
                stage_dil(k, l)
            if stages is None or "sb" in stages:
                stage_sb(k, l)
            if stages is None or "s5" in stages:
                stage_s5(k, l)
            if stages is None or "merge" in stages:
                stage_merge(k, l, cur, nxt)
                cur, nxt = nxt, cur
            if stages is None or "ffn" in stages:
                stage_ffn(k, l, cur, nxt)
                cur, nxt = nxt, cur
        if stages is None or "final" in stages:
            stage_final(k, cur)
        P.barrier()
        P.build()
    return nc


def stage_consts(k):
    P, A, I = k.P, k.A, k.I
    k.ident_f = A.f32(128)
    k.ident_b = A.bf16(128)
    k.ones_b = A.bf16(128)
    k.negU_b = A.bf16(128)
    k.negones_b = A.bf16(128)
    k.iota_f = A.f32(128)
    k.negones_f = A.f32(128)
    k.negident_b = A.bf16(128)
    k.gcols = A.f32(8 * 9)
    k.CB = Buf("consts")
    s = k.sem("const")
    P.dma("sp", k.ident_f, I["ident"], s, writes=[k.CB])
    P.dma("pool", k.ident_b, I["ident"], k.sem("const2"), writes=[k.CB])
    P.dma("pool", k.negU_b, I["negU"], k.sem("const2"), writes=[k.CB])
    P.dma("sp", k.iota_f, I["iota"], s, writes=[k.CB])
    gv = r3(k.gcols, 9)
    for l in range(NL):
        P.dma("sp", gv[:, l, :], I["attn_norm_g"][l].rearrange("(kc p) -> p kc", p=128), s, writes=[k.CB], allow_slow_non_contiguous=True)
        P.dma("sp", gv[:, 4 + l, :], I["ffn_norm_g"][l].rearrange("(kc p) -> p kc", p=128), s, writes=[k.CB], allow_slow_non_contiguous=True)
    P.dma("sp", gv[:, 8, :], I["final_norm_g"].rearrange("(kc p) -> p kc", p=128), s, writes=[k.CB], allow_slow_non_contiguous=True)
    P.op("dve", lambda e: e.memset(k.ones_b, 1.0), writes=[k.CB])
    P.op("dve", lambda e: e.memset(k.negones_b, -1.0), writes=[k.CB])
    P.op("dve", lambda e: e.memset(k.negones_f, -1.0), writes=[k.CB])
    P.op("dve", lambda e: e.tensor_scalar_mul(out=k.negident_b, in0=k.ident_b, scalar1=-1.0), writes=[k.CB])
    k.const_end = A.p
    P.barrier()


def psum_rot(k, idxs):
    state = {"i": 0}

    def nxt():
        i = idxs[state["i"] % len(idxs)]
        state["i"] += 1
        return k.pb[i], k.pbB[i]
    return nxt


def rmsnorm_supertile(k, xs, xsB, gidx, xn_out, xn_B, sq, sqB, rt, rtB, bank, out_engines=("dve",)):
    P = k.P
    pbt, pbB = bank
    gv = r3(k.gcols, 9)
    P.op("act", lambda e: e.activation(out=sq, in_=xs, func=AF.Square), reads=[xsB], writes=[sqB])
    fns = []
    for kc in range(8):
        fns.append(lambda e, kc=kc: e.matmul(pbt[:, :], lhsT=k.ones_b, rhs=sq[:, kc, :], start=(kc == 0), stop=(kc == 7)))
    P.op("pe", fns, reads=[sqB, k.CB], writes=[pbB])
    P.op("act", lambda e: e.activation(out=rt, in_=pbt[:, :], func=AF.Sqrt, bias=EPS, scale=1.0 / D), reads=[pbB], writes=[rtB])
    P.op("dve", lambda e: e.reciprocal(out=rt, in_=rt), reads=[rtB], writes=[rtB])
    for kc in range(8):
        eng = out_engines[kc % len(out_engines)]
        P.op(eng, lambda e, kc=kc: e.scalar_tensor_tensor(out=xn_out[:, kc, :], in0=xs[:, kc, :], scalar=gv[:, gidx, kc:kc + 1],
                                                         in1=rt, op0=ALU.mult, op1=ALU.mult),
             reads=[xsB, rtB, k.CB], writes=[xn_B])


def stage_load_x(k, cur):
    P, A, I, O = k.P, k.A, k.I, k.O
    A.p = k.const_end
    xin = [A.f32(1024) for _ in range(2)]
    xinB = [Buf("xin%d" % i) for i in range(2)]
    xs = [r3(A.f32(4096), 8) for _ in range(2)]
    xsB = [Buf("xs%d" % i) for i in range(2)]
    rot = psum_rot(k, [0, 1, 2, 3])
    dst = O[cur].rearrange("(kc p) t -> p kc t", p=128)
    for s in range(NS):
        for tt in range(4):
            t = s * 4 + tt
            xi, xiB = xin[t % 2], xinB[t % 2]
            P.dma("sp", xi, I["x"][t * 128:(t + 1) * 128, :], k.sem("lx%d" % (t % 2)), writes=[xiB])
            for half in range(2):
                pbt, pbB = rot()
                fns = [lambda e, q=q, pbt=pbt, xi=xi, half=half: e.transpose(out=pbt[:, q * 128:(q + 1) * 128],
                                                                             in_=xi[:, (half * 4 + q) * 128:(half * 4 + q + 1) * 128],
                                                                             identity=k.ident_f) for q in range(4)]
                P.op("pe", fns, reads=[xiB, k.CB], writes=[pbB])
                eng = "dve" if half == 0 else "act"
                dstv = xs[s % 2][:, half * 4:(half + 1) * 4, tt * 128:(tt + 1) * 128]
                srcv = r3(pbt[:, :], 4)
                if eng == "dve":
                    P.op("dve", lambda e, dstv=dstv, srcv=srcv: e.tensor_copy(out=dstv, in_=srcv), reads=[pbB], writes=[xsB[s % 2]])
                else:
                    P.op("act", lambda e, dstv=dstv, srcv=srcv: e.activation(out=dstv, in_=srcv, func=AF.Copy), reads=[pbB], writes=[xsB[s % 2]])
        P.dma("sp", dst[:, :, s * 512:(s + 1) * 512], xs[s % 2], k.sem("sx%d" % (s % 2)), reads=[xsB[s % 2]], writes=[k.DB[cur]])
    P.barrier()


def stage_A(k, l, cur):
    P, A, I, O = k.P, k.A, k.I, k.O
    A.p = k.const_end
    xnT = r3(A.bf16(8 * S), 8)
    xnB = [Buf("xn%d" % s) for s in range(NS)]
    p_after_xn = A.p
    xs = [r3(A.f32(4096), 8) for _ in range(2)]
    xsB = [Buf("xs%d" % i) for i in range(2)]
    sq = r3(A.bf16(4096), 8)
    sqB = Buf("sq")
    rt = A.f32(512)
    rtB = Buf("rt")
    src = O[cur].rearrange("(kc p) t -> p kc t", p=128)
    for s in range(NS):
        P.dma("sp", xs[s % 2], src[:, :, s * 512:(s + 1) * 512], k.sem("ax%d" % (s % 2)), reads=[k.DB[cur]], writes=[xsB[s % 2]])
        rmsnorm_supertile(k, xs[s % 2], xsB[s % 2], l, xnT[:, :, s * 512:(s + 1) * 512], xnB[s], sq, sqB, rt, rtB, (k.pb[7], k.pbB[7]))
    w_in = I["w_in"][l]
    jobs = []
    jobs.append((w_in[:, 0:512], O["upool"], "f32", 1.0, "upool"))
    jobs.append((w_in[:, 512:1280], O["qkT"][0:768], "bf", 0.125, "qkT"))
    jobs.append((w_in[:, 1280:2048], O["qkT"][768:1536], "bf", 1.0, "qkT"))
    jobs.append((w_in[:, 2816:3328], O["qkT"][1536:2048], "bf", 128.0 ** -0.5, "qkT"))
    jobs.append((w_in[:, 3328:3840], O["qkT"][2048:2560], "bf", 1.0, "qkT"))
    jobs.append((w_in[:, 4352:4864], O["qkT"][2560:3072], "bf", 1.0, "qkT"))
    for b in range(4):
        jobs.append((I["w_gate"][l, b], b * 8, "sig", 1.0, "gate"))
    wsl = [r3(A.bf16(8 * 512), 8) for _ in range(2)]
    wslB = [Buf("wsl%d" % i) for i in range(2)]
    ot_b = [A.bf16(S) for _ in range(2)]
    ot_f = A.f32(S)
    otB = [Buf("ot%d" % i) for i in range(3)]
    rot = psum_rot(k, [0, 1, 2, 3, 4, 5])
    slab_i = 0
    out_i = 0
    ev_i = 0
    for (wap, dest, mode, scale, dname) in jobs:
        ncols = wap.shape[1]
        wv = wap.rearrange("(kc p) n -> p kc n", p=128)
        for c0 in range(0, ncols, 512):
            cw = min(512, ncols - c0)
            sl, slB = wsl[slab_i % 2], wslB[slab_i % 2]
            P.dma("pool", sl[:, :, 0:cw], wv[:, :, c0:c0 + cw], k.sem("wsl%d" % (slab_i % 2)), writes=[slB])
            slab_i += 1
            for cc in range(cw // 128):
                if mode == "f32":
                    ot, oB, osem = ot_f, otB[2], "ot2"
                else:
                    ot, oB, osem = ot_b[out_i % 2], otB[out_i % 2], "ot%d" % (out_i % 2)
                    out_i += 1
                for s in range(NS):
                    pbt, pbB = rot()
                    fns = [lambda e, kc=kc, pbt=pbt, sl=sl, cc=cc, s=s: e.matmul(pbt[:, :], lhsT=sl[:, kc, cc * 128:(cc + 1) * 128],
                                                                                rhs=xnT[:, kc, s * 512:(s + 1) * 512],
                                                                                start=(kc == 0), stop=(kc == 7)) for kc in range(8)]
                    P.op("pe", fns, reads=[slB, xnB[s]], writes=[pbB])
                    ov = ot[:, s * 512:(s + 1) * 512]
                    if mode == "sig":
                        P.op("act", lambda e, ov=ov, pbt=pbt: e.activation(out=ov, in_=pbt[:, :], func=AF.Sigmoid), reads=[pbB], writes=[oB])
                    else:
                        if ev_i % 2 == 0:
                            P.op("act", lambda e, ov=ov, pbt=pbt, scale=scale: e.activation(out=ov, in_=pbt[:, :], func=AF.Copy, scale=scale), reads=[pbB], writes=[oB])
                        else:
                            P.op("dve", lambda e, ov=ov, pbt=pbt, scale=scale: e.tensor_scalar(out=ov, in0=pbt[:, :], scalar1=scale, scalar2=None, op0=ALU.mult), reads=[pbB], writes=[oB])
                        ev_i += 1
                r0 = c0 + cc * 128
                if mode == "sig":
                    rch = dest + r0 // 128
                    P.dma("sp", O["gate"][:, :, rch, :].rearrange("s p t -> p s t"), ot.rearrange("p (s t) -> p s t", s=NS), k.sem(osem), reads=[oB], writes=[k.DB[dname]])
                else:
                    P.dma("sp", dest[r0:r0 + 128, :], ot, k.sem(osem), reads=[oB], writes=[k.DB[dname]])
    P.barrier()
    A.p = p_after_xn
    wv_sb = r3(A.bf16(8 * 1280), 8)
    wvB = Buf("wv")
    wview = w_in.rearrange("(kc p) n -> p kc n", p=128)
    P.dma("pool", wv_sb[:, :, 0:768], wview[:, :, 2048:2816], k.sem("wv"), writes=[wvB])
    P.dma("pool", wv_sb[:, :, 768:1280], wview[:, :, 3840:4352], k.sem("wv"), writes=[wvB])
    vd = A.bf16(12 * NT * 64).rearrange("p (h t e) -> p h t e", h=12, t=NT)
    vs = A.bf16(4 * NT * 128).rearrange("p (h t e) -> p h t e", h=4, t=NT)
    vallB = Buf("vall")
    for t in range(NT):
        s = t // 4
        for gi, (n0, n1) in enumerate(((0, 512), (512, 768), (768, 1280))):
            pbt, pbB = rot()
            fns = [lambda e, kc=kc, pbt=pbt, n0=n0, n1=n1, t=t: e.matmul(pbt[:, 0:n1 - n0], lhsT=xnT[:, kc, t * 128:(t + 1) * 128],
                                                                         rhs=wv_sb[:, kc, n0:n1], start=(kc == 0), stop=(kc == 7)) for kc in range(8)]
            P.op("pe", fns, reads=[wvB, xnB[s]], writes=[pbB])
            if gi == 0:
                ov, iv = vd[:, 0:8, t, :], pbt[:, 0:512].rearrange("p (h e) -> p h e", h=8)
                P.op("dve", lambda e, ov=ov, iv=iv: e.tensor_copy(out=ov, in_=iv), reads=[pbB], writes=[vallB])
            elif gi == 1:
                ov, iv = vd[:, 8:12, t, :], pbt[:, 0:256].rearrange("p (h e) -> p h e", h=4)
                P.op("act", lambda e, ov=ov, iv=iv: e.activation(out=ov, in_=iv, func=AF.Copy), reads=[pbB], writes=[vallB])
            else:
                ov, iv = vs[:, 0:4, t, :], pbt[:, 0:512].rearrange("p (h e) -> p h e", h=4)
                P.op("act", lambda e, ov=ov, iv=iv: e.activation(out=ov, in_=iv, func=AF.Copy), reads=[pbB], writes=[vallB])
    for h in range(12):
        P.dma("sp", O["vdil"][h], vd[:, h].rearrange("p t e -> p (t e)"), k.sem("vst"), reads=[vallB], writes=[k.DB["vdil"]])
    for h in range(4):
        P.dma("sp", O["vsb"][h], vs[:, h].rearrange("p t e -> p (t e)"), k.sem("vst"), reads=[vallB], writes=[k.DB["vsb"]])
    P.barrier()


def stage_pool(k, l):
    P, A, I, O = k.P, k.A, k.I, k.O
    A.p = k.const_end
    PADC = 16
    pw = r3(A.bf16(4 * 128), 4)
    pwB = Buf("pw")
    P.dma("pool", pw, I["pool_w"][l].rearrange("g c d -> c g d"), k.sem("pw"), writes=[pwB])
    psc = A.f32(4)
    pinv = A.f32(64)
    P.dma("sp", psc, I["pool_scale"][l].rearrange("(g p) -> p g", p=128), k.sem("pw2"), writes=[pwB], allow_slow_non_contiguous=True)
    P.dma("sp", pinv, I["poolinv"], k.sem("pw2"), writes=[pwB])
    u = [A.f32(PADC + S) for _ in range(2)]
    uB = [Buf("u%d" % i) for i in range(2)]
    s1 = A.f32(PADC + S)
    s2 = A.f32(PADC + S)
    sB = [Buf("s1"), Buf("s2")]
    pb16 = A.bf16(S)
    pb16B = Buf("pb16")
    yo = [A.bf16(S) for _ in range(2)]
    yoB = [Buf("yo%d" % i) for i in range(2)]
    rot = psum_rot(k, [0, 1, 2, 3])
    for i in range(2):
        P.op("dve", lambda e, i=i: e.memset(u[i][:, 0:PADC], 0.0), writes=[uB[i]])
    P.op("dve", lambda e: e.memset(s1[:, 0:PADC], 0.0), writes=[sB[0]])
    P.op("dve", lambda e: e.memset(s2[:, 0:PADC], 0.0), writes=[sB[1]])
    for g, w in enumerate((2, 4, 8, 16)):
        ug, ugB = u[g % 2], uB[g % 2]
        P.dma("sp", ug[:, PADC:], O["upool"][g * 128:(g + 1) * 128, :], k.sem("pu%d" % (g % 2)), reads=[k.DB["upool"]], writes=[ugB])
        src, srcB = ug, ugB
        sh = 1
        bufs = [(s1, sB[0]), (s2, sB[1])]
        bi = 0
        while sh < w:
            dst, dstB = bufs[bi % 2]
            eng = "dve"
            P.op(eng, lambda e, dst=dst, src=src, sh=sh: e.tensor_add(out=dst[:, PADC:], in0=src[:, PADC:], in1=src[:, PADC - sh:PADC + S - sh]),
                 reads=[srcB], writes=[dstB])
            src, srcB = dst, dstB
            bi += 1
            sh *= 2
        dst, dstB = bufs[bi % 2]
        P.op("dve", lambda e, dst=dst, src=src, ug=ug, w=w: e.scalar_tensor_tensor(out=dst[:, PADC:], in0=src[:, PADC:], scalar=1.0 / w, in1=ug[:, PADC:],
                                                                            op0=ALU.mult, op1=ALU.subtract),
             reads=[srcB, ugB], writes=[dstB])
        tmpc = A.f32(16)
        tB = Buf("tmpc")
        P.op("dve", lambda e, src=src, g=g, tmpc=tmpc: e.tensor_mul(out=tmpc, in0=src[:, PADC:PADC + 16], in1=pinv[:, g * 16:(g + 1) * 16]),
             reads=[srcB, pwB], writes=[tB])
        P.op("dve", lambda e, dst=dst, ug=ug, tmpc=tmpc: e.tensor_sub(out=dst[:, PADC:PADC + 16], in0=tmpc, in1=ug[:, PADC:PADC + 16]),
             reads=[tB, ugB], writes=[dstB])
        P.op("act", lambda e, dst=dst: e.activation(out=pb16, in_=dst[:, PADC:], func=AF.Copy), reads=[dstB], writes=[pb16B])
        y_t, y_B = yo[g % 2], yoB[g % 2]
        for s in range(NS):
            pbt, pbB = rot()
            P.op("pe", lambda e, pbt=pbt, g=g, s=s: e.matmul(pbt[:, :], lhsT=pw[:, g, :], rhs=pb16[:, s * 512:(s + 1) * 512], start=True, stop=True),
                 reads=[pwB, pb16B], writes=[pbB])
            P.op("dve", lambda e, pbt=pbt, g=g, s=s, y_t=y_t: e.tensor_scalar(out=y_t[:, s * 512:(s + 1) * 512], in0=pbt[:, :], scalar1=psc[:, g:g + 1], scalar2=None, op0=ALU.mult),
                 reads=[pbB, pwB], writes=[y_B])
        P.dma("sp", O["ybrT"][:, :, g, :].rearrange("s p t -> p s t"), y_t.rearrange("p (s t) -> p s t", s=NS), k.sem("py%d" % (g % 2)), reads=[y_B], writes=[k.DB["ybrT"]])
    P.barrier()


def stage_dil(k, l):
    P, A, I, O = k.P, k.A, k.I, k.O
    A.p = k.const_end
    bl = dil_block_list()
    nb = len(bl)
    tab = r3(A.bf16(nb * 128), nb)
    tabB = Buf("tab")
    qT = r3(A.bf16(3 * S), 3)
    kT = r3(A.bf16(3 * S), 3)
    qkB = Buf("qk")
    vv = A.bf16(NT * 3 * 64).rearrange("p (g t e) -> p g t e", t=NT, g=3)
    vB = Buf("v")
    pm = [A.bf16(512) for _ in range(3)]
    pmB = [Buf("pm%d" % i) for i in range(3)]
    rd = A.f32(128)
    rdB = Buf("rd")
    yd = [A.bf16(S) for _ in range(2)]
    ydB = [Buf("yd%d" % i) for i in range(2)]
    srot = psum_rot(k, [0, 1, 2])
    orot = psum_rot(k, [3, 4, 5, 6])
    for j in range(4):
        P.dma("pool", tab, I["dil_tab"][j].rearrange("b c i -> c b i"), k.sem("dtab"), writes=[tabB])
        for g in range(3):
            h = 4 * g + j
            P.dma("sp", qT[0:64, g, :], O["qkT"][h * 64:(h + 1) * 64, :], k.sem("dq"), reads=[k.DB["qkT"]], writes=[qkB])
            P.dma("sp", kT[0:64, g, :], O["qkT"][768 + h * 64:768 + (h + 1) * 64, :], k.sem("dq"), reads=[k.DB["qkT"]], writes=[qkB])
            P.dma("sp", vv[:, g].rearrange("p t e -> p (t e)"), O["vdil"][h], k.sem("dv"), reads=[k.DB["vdil"]], writes=[vB])
        y_t, y_B = yd[j % 2], ydB[j % 2]
        items = []
        for n in range(NT):
            blocks = [(bi, g, n - delta) for bi, (g, delta) in enumerate(bl) if n - delta >= 0]
            nblk = len(blocks)
            ngrp = (nblk + 3) // 4
            for gi_, c0 in enumerate(range(0, nblk, 4)):
                items.append((n, gi_, ngrp, nblk, c0, blocks[c0:c0 + 4]))
        obanks = [((k.pb[3], k.pbB[3]), (k.pb[4], k.pbB[4])), ((k.pb[5], k.pbB[5]), (k.pb[6], k.pbB[6]))]
        sbanks = [(k.pb[0], k.pbB[0]), (k.pb[1], k.pbB[1]), (k.pb[2], k.pbB[2])]

        def d_scores(t):
            n, gi_, ngrp, nblk, c0, grp = items[t]
            pst, psB = sbanks[t % 3]
            fns = []
            m = 0
            while m < len(grp):
                m2 = m
                while m2 + 1 < len(grp) and grp[m2 + 1][0] == grp[m2][0] + 1:
                    m2 += 1
                bi0 = grp[m][0]
                nrun = m2 - m + 1
                fns.append(lambda e, m=m, bi0=bi0, nrun=nrun: e.matmul(
                    pst[:, m * 128:(m + nrun) * 128], lhsT=k.ident_b, rhs=tab[:, bi0:bi0 + nrun, :].rearrange("p b i -> p (b i)"), start=True, stop=False))
                for mm_ in range(m, m2 + 1):
                    _, g, nk = grp[mm_]
                    fns.append(lambda e, mm_=mm_, g=g, nk=nk, lastrun=(mm_ == m2): e.matmul(pst[:, mm_ * 128:(mm_ + 1) * 128], lhsT=kT[0:64, g, nk * 128:(nk + 1) * 128],
                                                                                          rhs=qT[0:64, g, n * 128:(n + 1) * 128], start=False, stop=lastrun))
                m = m2 + 1
            P.op("pe", fns, reads=[tabB, qkB, k.CB], writes=[psB])
            p_t, p_B = pm[t % 3], pmB[t % 3]
            wd = len(grp) * 128
            P.op("act", lambda e: e.activation(out=p_t[:, 0:wd], in_=pst[:, 0:wd], func=AF.Exp), reads=[psB], writes=[p_B])

        def d_pv(t):
            n, gi_, ngrp, nblk, c0, grp = items[t]
            (onum, onumB), (oden, odenB) = obanks[n % 2]
            p_t, p_B = pm[t % 3], pmB[t % 3]
            wd = len(grp) * 128
            fns = []
            for m, (bi, g, nk) in enumerate(grp):
                first = (c0 + m == 0)
                last = (c0 + m == nblk - 1)
                fns.append(lambda e, m=m, g=g, nk=nk, first=first, last=last: e.matmul(
                    onum[0:64, 0:128], lhsT=vv[:, g, nk, :], rhs=p_t[:, m * 128:(m + 1) * 128], start=first, stop=last))
            fns.append(lambda e: e.matmul(oden[0:64, 0:wd], lhsT=k.ones_b[:, 0:64], rhs=p_t[:, 0:wd], start=(gi_ == 0), stop=(gi_ == ngrp - 1)))
            P.op("pe", fns, reads=[p_B, vB, k.CB], writes=[onumB, odenB])
            if gi_ == ngrp - 1:
                W1 = min(nblk, 4) * 128
                if W1 > 128:
                    P.op("dve", lambda e: e.reduce_sum(out=rd[0:64, :], in_=oden[0:64, 0:W1].rearrange("p (m i) -> p i m", m=W1 // 128),
                                                        axis=mybir.AxisListType.X), reads=[odenB], writes=[rdB])
                    P.op("dve", lambda e: e.reciprocal(out=rd[0:64, :], in_=rd[0:64, :]), reads=[rdB], writes=[rdB])
                else:
                    P.op("dve", lambda e: e.reciprocal(out=rd[0:64, :], in_=oden[0:64, 0:128]), reads=[odenB], writes=[rdB])
                P.op("dve", lambda e, y_t=y_t: e.tensor_mul(out=y_t[0:64, n * 128:(n + 1) * 128], in0=onum[0:64, 0:128], in1=rd[0:64, :]),
                     reads=[onumB, rdB], writes=[y_B])

        G = len(items)
        for t in range(-1, G):
            if t + 1 < G:
                d_scores(t + 1)
            if t >= 0:
                d_pv(t)
        P.dma("sp", O["ybrT"][:, (j % 2) * 64:(j % 2) * 64 + 64, 4 + j // 2, :].rearrange("s p t -> p s t"), y_t[0:64, :].rearrange("p (s t) -> p s t", s=NS),
              k.sem("dy%d" % (j % 2)), reads=[y_B], writes=[k.DB["ybrT"]])
    P.barrier()


def stage_sb(k, l):
    P, A, I, O = k.P, k.A, k.I, k.O
    A.p = k.const_end
    msk = r3(A.bf16(4 * 512), 4)
    mskB = Buf("msk")
    P.dma("pool", msk, I["sbmask"].rearrange("d s t -> s d t"), k.sem("sbm"), writes=[mskB])
    qT = [A.bf16(S) for _ in range(4)]
    kT = [A.bf16(S) for _ in range(4)]
    vv = [r3(A.bf16(NT * 128), NT) for _ in range(4)]
    hB = [Buf("sbh%d" % i) for i in range(4)]
    e1 = [A.f32(512) for _ in range(2)]
    e1B = [Buf("e1%d" % i) for i in range(2)]
    sp = [A.bf16(512) for _ in range(3)]
    spB = [Buf("sp%d" % i) for i in range(3)]
    aa = [A.bf16(512) for _ in range(2)]
    aaB = [Buf("aa%d" % i) for i in range(2)]
    ssf = [A.f32(512) for _ in range(2)]
    ssfB = [Buf("ssf%d" % i) for i in range(2)]
    yo = [A.bf16(S) for _ in range(2)]
    yoB = [Buf("sby%d" % i) for i in range(2)]
    zb = [(k.pb[0], k.pbB[0]), (k.pb[1], k.pbB[1])]
    wb = [(k.pb[2], k.pbB[2]), (k.pb[3], k.pbB[3])]
    ob = [(k.pb[4], k.pbB[4]), (k.pb[5], k.pbB[5])]
    for h in range(4):
        P.dma("sp", qT[h], O["qkT"][1536 + h * 128:1536 + (h + 1) * 128, :], k.sem("sbq%d" % h), reads=[k.DB["qkT"]], writes=[hB[h]])
        P.dma("sp", kT[h], O["qkT"][2048 + h * 128:2048 + (h + 1) * 128, :], k.sem("sbq%d" % h), reads=[k.DB["qkT"]], writes=[hB[h]])
        P.dma("sp", vv[h].rearrange("p t e -> p (t e)"), O["vsb"][h], k.sem("sbq%d" % h), reads=[k.DB["vsb"]], writes=[hB[h]])
    pairs = []
    c = 0
    for h in range(4):
        for T in range(NS):
            nblk = 4 * T + 4
            for idx, b in enumerate(range(nblk - 1, -1, -1)):
                pairs.append((h, T, idx, b, nblk, b - 4 * T, c))
            c += 1
    N = len(pairs)

    def s1(i):
        h, T, idx, b, nblk, diag, c = pairs[i]
        zt, zB = zb[i % 2]
        kv = kT[h][:, b * 128:(b + 1) * 128]
        qv = qT[h][:, T * 512:(T + 1) * 512]
        P.op("pe", lambda e: e.matmul(zt[:, :], lhsT=kv, rhs=qv, start=True, stop=True), reads=[hB[h]], writes=[zB])

    def s2(i):
        h, T, idx, b, nblk, diag, c = pairs[i]
        zt, zB = zb[i % 2]
        e_t, e_B = e1[i % 2], e1B[i % 2]
        s_t, s_B = sp[i % 3], spB[i % 3]
        P.op("act", lambda e: e.activation(out=e_t, in_=zt[:, :], func=AF.Exp), reads=[zB], writes=[e_B])
        P.op("act", lambda e: e.activation(out=s_t, in_=e_t, func=AF.Ln, bias=1.0), reads=[e_B], writes=[s_B])
        if diag >= 0:
            P.op("dve", lambda e: e.tensor_mul(out=s_t, in0=s_t, in1=msk[:, diag, :]), reads=[s_B, mskB], writes=[s_B])

    def s3(i):
        h, T, idx, b, nblk, diag, c = pairs[i]
        wt, wB = wb[i % 2]
        s_t, s_B = sp[i % 3], spB[i % 3]
        kv = kT[h][:, b * 128:(b + 1) * 128]
        qv = qT[h][:, T * 512:(T + 1) * 512]
        fns = [lambda e: e.matmul(wt[:, :], lhsT=kv, rhs=qv, start=True, stop=False),
               lambda e: e.matmul(wt[:, :], lhsT=k.negU_b, rhs=s_t, start=False, stop=(idx == 0))]
        rds = [hB[h], s_B, k.CB]
        if idx > 0:
            sf_prev, sf_prevB = ssf[(idx - 1) % 2], ssfB[(idx - 1) % 2]
            fns.append(lambda e: e.matmul(wt[:, :], lhsT=k.negones_f, rhs=sf_prev, start=False, stop=True))
            rds.append(sf_prevB)
        P.op("pe", fns, reads=rds, writes=[wB])
        if idx < nblk - 1:
            if idx == 0:
                P.op("dve", lambda e: e.tensor_copy(out=ssf[0], in_=s_t), reads=[s_B], writes=[ssfB[0]])
            else:
                P.op("dve", lambda e: e.tensor_add(out=ssf[idx % 2], in0=ssf[(idx - 1) % 2], in1=s_t),
                     reads=[s_B, ssfB[(idx - 1) % 2]], writes=[ssfB[idx % 2]])

    def s4(i):
        h, T, idx, b, nblk, diag, c = pairs[i]
        wt, wB = wb[i % 2]
        a_t, a_B = aa[i % 2], aaB[i % 2]
        P.op("act", lambda e: e.activation(out=a_t, in_=wt[:, :], func=AF.Exp), reads=[wB], writes=[a_B])
        if diag >= 0:
            P.op("dve", lambda e: e.tensor_mul(out=a_t, in0=a_t, in1=msk[:, diag, :]), reads=[a_B, mskB], writes=[a_B])

    def s5(i):
        h, T, idx, b, nblk, diag, c = pairs[i]
        a_t, a_B = aa[i % 2], aaB[i % 2]
        ot, oB = ob[c % 2]
        P.op("pe", lambda e: e.matmul(ot[:, :], lhsT=vv[h][:, b, :], rhs=a_t, start=(idx == 0), stop=(idx == nblk - 1)),
             reads=[hB[h], a_B], writes=[oB])
        if idx == nblk - 1:
            y_t, y_B = yo[h % 2], yoB[h % 2]
            if T % 2 == 0:
                P.op("act", lambda e: e.activation(out=y_t[:, T * 512:(T + 1) * 512], in_=ot[:, :], func=AF.Copy), reads=[oB], writes=[y_B])
            else:
                P.op("dve", lambda e: e.tensor_copy(out=y_t[:, T * 512:(T + 1) * 512], in_=ot[:, :]), reads=[oB], writes=[y_B])
            if T == NS - 1:
                P.dma("sp", O["ybrT"][:, :, 6 + h, :].rearrange("s p t -> p s t"), y_t.rearrange("p (s t) -> p s t", s=NS), k.sem("sby%d" % (h % 2)), reads=[y_B], writes=[k.DB["ybrT"]])

    for t in range(-2, N):
        if 0 <= t + 2 < N:
            s1(t + 2)
            s2(t + 2)
        if 0 <= t + 1 < N:
            s3(t + 1)
            s4(t + 1)
        if 0 <= t < N:
            s5(t)
    P.barrier()


def stage_s5(k, l):
    P, A, I, O = k.P, k.A, k.I, k.O
    A.p = k.const_end
    PI = math.pi
    sm = lambda: A.f32(16)
    are, aim, ldt, dtv, rr, th, t0, t1, sn, cs = [sm() for _ in range(10)]
    lbr, lbi, nr, den, cre, cim, ncim, ere, eim, neim = [sm() for _ in range(10)]
    xre, xim = sm(), sm()
    t2x = sm()
    dcol = A.f32(4)
    prmB = Buf("s5prm")
    sp_ = k.sem("s5p")
    P.dma("sp", are, I["s5are"][l], sp_, writes=[prmB])
    P.dma("sp", aim, I["s5aim"][l], sp_, writes=[prmB])
    P.dma("sp", ldt, I["s5ldt"][l], sp_, writes=[prmB])
    P.dma("sp", dcol, I["s5_d"][l].rearrange("(kc p) -> p kc", p=128), sp_, writes=[prmB], allow_slow_non_contiguous=True)

    def pop(eng, f):
        P.op(eng, f, reads=[prmB, k.CB], writes=[prmB])

    MAGIC = 12582912.0
    INV2PI = 1.0 / (2 * PI)

    def sin_reduced(opf, x_ap, out_ap, tmp, tmp2, shift):
        if shift != 0.0:
            opf("dve", lambda e: e.tensor_scalar_add(out=tmp2, in0=x_ap, scalar1=shift))
            xs = tmp2
        else:
            xs = x_ap
        opf("dve", lambda e: e.tensor_scalar(out=tmp, in0=xs, scalar1=INV2PI, scalar2=MAGIC, op0=ALU.mult, op1=ALU.add))
        opf("dve", lambda e: e.tensor_scalar_add(out=tmp, in0=tmp, scalar1=-MAGIC))
        opf("dve", lambda e: e.scalar_tensor_tensor(out=tmp, in0=tmp, scalar=-2 * PI, in1=xs, op0=ALU.mult, op1=ALU.add))
        opf("dve", lambda e: e.tensor_scalar(out=tmp, in0=tmp, scalar1=PI, scalar2=-PI, op0=ALU.min, op1=ALU.max))
        opf("act", lambda e: e.activation(out=out_ap, in_=tmp, func=AF.Sin))

    def sincos(th_ap, s_out, c_out, tmp):
        sin_reduced(pop, th_ap, s_out, tmp, t2x, 0.0)
        sin_reduced(pop, th_ap, c_out, tmp, t2x, 0.5 * PI)

    pop("act", lambda e: e.activation(out=dtv, in_=ldt, func=AF.Exp))
    pop("dve", lambda e: e.tensor_mul(out=t0, in0=are, in1=dtv))
    pop("act", lambda e: e.activation(out=rr, in_=t0, func=AF.Exp))
    pop("dve", lambda e: e.tensor_mul(out=th, in0=aim, in1=dtv))
    sincos(th, sn, cs, t0)
    pop("dve", lambda e: e.tensor_mul(out=lbr, in0=rr, in1=cs))
    pop("dve", lambda e: e.tensor_mul(out=lbi, in0=rr, in1=sn))
    pop("dve", lambda e: e.tensor_scalar_add(out=nr, in0=lbr, scalar1=-1.0))
    pop("dve", lambda e: e.tensor_mul(out=den, in0=are, in1=are))
    pop("dve", lambda e: e.tensor_mul(out=t0, in0=aim, in1=aim))
    pop("dve", lambda e: e.tensor_add(out=den, in0=den, in1=t0))
    pop("dve", lambda e: e.reciprocal(out=den, in_=den))
    pop("dve", lambda e: e.tensor_mul(out=t0, in0=nr, in1=are))
    pop("dve", lambda e: e.tensor_mul(out=t1, in0=lbi, in1=aim))
    pop("dve", lambda e: e.tensor_add(out=t0, in0=t0, in1=t1))
    pop("dve", lambda e: e.tensor_mul(out=cre, in0=t0, in1=den))
    pop("dve", lambda e: e.tensor_mul(out=t0, in0=lbi, in1=are))
    pop("dve", lambda e: e.tensor_mul(out=t1, in0=nr, in1=aim))
    pop("dve", lambda e: e.tensor_sub(out=t0, in0=t0, in1=t1))
    pop("dve", lambda e: e.tensor_mul(out=cim, in0=t0, in1=den))
    pop("dve", lambda e: e.tensor_scalar_mul(out=ncim, in0=cim, scalar1=-1.0))
    pop("dve", lambda e: e.tensor_scalar_mul(out=t1, in0=th, scalar1=512.0))
    sincos(t1, eim, ere, t0)
    pop("dve", lambda e: e.tensor_scalar_mul(out=neim, in0=eim, scalar1=-1.0))
    pop("dve", lambda e: e.memset(xre, 0.0))
    pop("dve", lambda e: e.memset(xim, 0.0))
    tabB = Buf("s5tab")
    BUWr = r3(A.bf16(2048), 16)
    BUWi = r3(A.bf16(2048), 16)
    CWr = r3(A.bf16(2048), 16)
    CWi = r3(A.bf16(2048), 16)
    wB = Buf("s5w")
    P.dma("pool", CWr, I["s5cre"][l].rearrange("j s c -> s j c"), k.sem("s5c"), writes=[wB])
    P.dma("pool", CWi, I["s5cim"][l].rearrange("j s c -> s j c"), k.sem("s5c"), writes=[wB])
    P.op("dve", lambda e: e.tensor_scalar_mul(out=CWi, in0=CWi, scalar1=-1.0), reads=[wB], writes=[wB])
    nCWr = r3(A.bf16(2048), 16)
    P.op("dve", lambda e: e.tensor_scalar_mul(out=nCWr, in0=CWr, scalar1=-1.0), reads=[wB], writes=[wB])
    iota512 = A.f32(512)
    P.dma("sp", iota512, I["iota512"], k.sem("s5i"), writes=[wB])
    wglu = r3(A.bf16(4 * 1024), 4)
    P.dma("pool", wglu, I["s5_w_glu"][l].rearrange("(kc p) n -> p kc n", p=128), k.sem("s5g"), writes=[wB])
    bp = [(A.f32(128), A.f32(128)) for _ in range(2)]
    bpB = [Buf("bp%d" % i) for i in range(2)]
    bt = [A.f32(128) for _ in range(4)]
    btB = [Buf("bt%d" % i) for i in range(4)]
    rot = psum_rot(k, [0, 1, 2, 3])
    for j in range(16):
        (b_r, b_i), b_B = bp[j % 2], bpB[j % 2]
        P.dma("sp", b_r, I["s5bre"][l, j], k.sem("s5b%d" % (j % 2)), writes=[b_B])
        P.dma("sp", b_i, I["s5bim"][l, j], k.sem("s5b%d" % (j % 2)), writes=[b_B])
        tr, trB = bt[(2 * j) % 4], btB[(2 * j) % 4]
        ti, tiB = bt[(2 * j + 1) % 4], btB[(2 * j + 1) % 4]
        P.op("dve", lambda e, tr=tr, b_r=b_r, j=j: e.tensor_scalar(out=tr, in0=b_r, scalar1=cre[:, j:j + 1], scalar2=None, op0=ALU.mult), reads=[b_B, prmB], writes=[trB])
        P.op("dve", lambda e, tr=tr, b_i=b_i, j=j: e.scalar_tensor_tensor(out=tr, in0=b_i, scalar=ncim[:, j:j + 1], in1=tr, op0=ALU.mult, op1=ALU.add), reads=[b_B, prmB, trB], writes=[trB])
        P.op("dve", lambda e, ti=ti, b_i=b_i, j=j: e.tensor_scalar(out=ti, in0=b_i, scalar1=cre[:, j:j + 1], scalar2=None, op0=ALU.mult), reads=[b_B, prmB], writes=[tiB])
        P.op("dve", lambda e, ti=ti, b_r=b_r, j=j: e.scalar_tensor_tensor(out=ti, in0=b_r, scalar=cim[:, j:j + 1], in1=ti, op0=ALU.mult, op1=ALU.add), reads=[b_B, prmB, tiB], writes=[tiB])
        for (src, srcB, dstw) in ((tr, trB, BUWr), (ti, tiB, BUWi)):
            pbt, pbB = rot()
            P.op("pe", lambda e, pbt=pbt, src=src: e.transpose(out=pbt[:, 0:128], in_=src, identity=k.ident_f), reads=[srcB, k.CB], writes=[pbB])
            P.op("act", lambda e, pbt=pbt, dstw=dstw, j=j: e.activation(out=dstw[:, j, :], in_=pbt[:, 0:128], func=AF.Copy), reads=[pbB], writes=[wB])
    us5 = A.bf16(S)
    usB = Buf("us5")
    ygT = r3(A.bf16(4 * S), 4)
    ygB = Buf("ygT")
    cos5 = r3(A.f32(2048), 4)
    sin5 = r3(A.f32(2048), 4)
    R5 = r3(A.f32(2048), 4)
    p_main = A.p
    tin_reg = A.f32(8 * 512)
    tout_reg = A.f32(4 * 512)
    tin_bf = tin_reg.bitcast(BF16)
    tinb = [[tin_bf[:, (a * 4 + i) * 512:(a * 4 + i + 1) * 512] for i in range(4)] for a in range(2)]
    tinB = [[Buf("tin%d_%d" % (a, i)) for i in range(4)] for a in range(2)]
    tout_bf = tout_reg.bitcast(BF16)
    tout = [[tout_bf[:, (a * 4 + i) * 512:(a * 4 + i + 1) * 512] for i in range(4)] for a in range(2)]
    toutB = [[Buf("tout%d_%d" % (a, i)) for i in range(4)] for a in range(2)]
    ang = r3(tin_reg[:, 0:2048], 4)
    a2 = r3(tin_reg[:, 2048:4096], 4)
    a3 = r3(tout_reg[:, 0:2048], 4)
    bre = [A.f32(512) for _ in range(4)]
    bim = [A.f32(512) for _ in range(4)]
    gre = [A.f32(512) for _ in range(4)]
    gim = [A.f32(512) for _ in range(4)]
    breB = [Buf("bre%d" % i) for i in range(4)]
    bimB = [Buf("bim%d" % i) for i in range(4)]
    greB = [Buf("gre%d" % i) for i in range(4)]
    gimB = [Buf("gim%d" % i) for i in range(4)]
    XB = [Buf("X%d" % j) for j in range(16)]
    cx = [A.f32(2) for _ in range(4)]
    cxB = [Buf("cx%d" % i) for i in range(4)]
    ypre = A.f32(512)
    ypB = Buf("ypre")
    gu = A.f32(512)
    guB = Buf("gu")
    sg = A.f32(512)
    sgB = Buf("sg")
    brot = psum_rot(k, [0, 1, 2, 3])
    yrot = psum_rot(k, [6, 7])

    def tabop(eng, f):
        P.op(eng, f, reads=[tabB], writes=[tabB])
    for kc in range(4):
        P.barrier()
        for jj in range(4):
            j = 4 * kc + jj
            P.op("dve", lambda e, jj=jj, j=j: e.tensor_scalar(out=ang[:, jj, :], in0=iota512, scalar1=th[:, j:j + 1], scalar2=None, op0=ALU.mult),
                 reads=[prmB, wB], writes=[tabB])
            P.op("pool", lambda e, jj=jj, j=j: e.tensor_scalar(out=R5[:, jj, :], in0=iota512, scalar1=0.0, scalar2=rr[:, j:j + 1], op0=ALU.mult, op1=ALU.add),
                 reads=[prmB, wB], writes=[tabB])
        sin_reduced(tabop, ang, sin5, a2, a3, 0.0)
        sin_reduced(tabop, ang, cos5, a2, a3, 0.5 * PI)
        P.barrier()
        P.dma("sp", us5, O["qkT"][2560 + kc * 128:2560 + (kc + 1) * 128, :], k.sem("s5u"), reads=[k.DB["qkT"]], writes=[usB])
        for s in range(NS):
            usl = us5[:, s * 512:(s + 1) * 512]
            bu_banks = {}

            def e_bu(jj, usl=usl, kc=kc):
                j = 4 * kc + jj
                pr, prB = brot()
                pi_, piB = brot()
                bu_banks[jj] = (pr, prB, pi_, piB)
                P.op("pe", lambda e: e.matmul(pr[:, :], lhsT=BUWr[:, j, :], rhs=usl, start=True, stop=True), reads=[wB, usB], writes=[prB])
                P.op("pe", lambda e: e.matmul(pi_[:, :], lhsT=BUWi[:, j, :], rhs=usl, start=True, stop=True), reads=[wB, usB], writes=[piB])

            def e_prod(jj):
                pr, prB, pi_, piB = bu_banks[jj]
                cosB = cos5[:, jj, :]
                sinB = sin5[:, jj, :]
                tA, tB_, tC, tD = tinb[jj % 2]
                tAB, tBB, tCB, tDB = tinB[jj % 2]
                P.op("dve", lambda e: e.tensor_mul(out=tA, in0=pr[:, :], in1=cosB), reads=[prB, tabB], writes=[tAB])
                P.op("dve", lambda e: e.tensor_mul(out=tB_, in0=pi_[:, :], in1=sinB), reads=[piB, tabB], writes=[tBB])
                P.op("dve", lambda e: e.tensor_mul(out=tC, in0=pi_[:, :], in1=cosB), reads=[piB, tabB], writes=[tCB])
                P.op("dve", lambda e: e.tensor_mul(out=tD, in0=pr[:, :], in1=sinB), reads=[prB, tabB], writes=[tDB])

            def e_ident(jj):
                tA, tB_, tC, tD = tinb[jj % 2]
                tAB, tBB, tCB, tDB = tinB[jj % 2]
                bR, bRB = k.pb[4], k.pbB[4]
                bI, bIB = k.pb[5], k.pbB[5]
                P.op("pe", [lambda e: e.matmul(bR[:, :], lhsT=k.ident_b, rhs=tA, start=True, stop=False),
                            lambda e: e.matmul(bR[:, :], lhsT=k.ident_b, rhs=tB_, start=False, stop=True)], reads=[tAB, tBB, k.CB], writes=[bRB])
                P.op("pe", [lambda e: e.matmul(bI[:, :], lhsT=k.ident_b, rhs=tC, start=True, stop=False),
                            lambda e: e.matmul(bI[:, :], lhsT=k.negident_b, rhs=tD, start=False, stop=True)], reads=[tCB, tDB, k.CB], writes=[bIB])

            def e_scan(jj, kc=kc):
                j = 4 * kc + jj
                bR, bRB = k.pb[4], k.pbB[4]
                bI, bIB = k.pb[5], k.pbB[5]
                P.op("dve", lambda e: e.tensor_tensor_scan(out=gre[jj], data0=R5[:, jj, :], data1=bR[:, :], initial=xre[:, j:j + 1],
                                                          op0=ALU.mult, op1=ALU.add), reads=[bRB, XB[j], tabB], writes=[greB[jj]])
                P.op("dve", lambda e: e.tensor_tensor_scan(out=gim[jj], data0=R5[:, jj, :], data1=bI[:, :], initial=xim[:, j:j + 1],
                                                          op0=ALU.mult, op1=ALU.add), reads=[bIB, XB[j], tabB], writes=[gimB[jj]])

            e_bu(0); e_prod(0); e_bu(1); e_ident(0); e_prod(1); e_scan(0)
            e_bu(2); e_ident(1); e_prod(2); e_scan(1)
            e_bu(3); e_ident(2); e_prod(3); e_scan(2)
            e_ident(3); e_scan(3)
            for jj in range(4):
                j = 4 * kc + jj
                gl_r = gre[jj][:, 511:512]
                gl_i = gim[jj][:, 511:512]
                c_t, c_B = cx[jj], cxB[jj]
                P.op("dve", lambda e, gl_r=gl_r, j=j, c_t=c_t: e.tensor_tensor(out=c_t[:, 0:1], in0=gl_r, in1=ere[:, j:j + 1], op=ALU.mult), reads=[greB[jj], prmB], writes=[c_B])
                P.op("dve", lambda e, gl_i=gl_i, j=j, c_t=c_t: e.tensor_tensor(out=c_t[:, 1:2], in0=gl_i, in1=ere[:, j:j + 1], op=ALU.mult), reads=[gimB[jj], prmB], writes=[c_B])
                P.op("dve", lambda e, gl_i=gl_i, j=j, c_t=c_t: e.scalar_tensor_tensor(out=xre[:, j:j + 1], in0=gl_i, scalar=neim[:, j:j + 1], in1=c_t[:, 0:1], op0=ALU.mult, op1=ALU.add),
                     reads=[gimB[jj], c_B, prmB], writes=[XB[j]])
                P.op("dve", lambda e, gl_r=gl_r, j=j, c_t=c_t: e.scalar_tensor_tensor(out=xim[:, j:j + 1], in0=gl_r, scalar=eim[:, j:j + 1], in1=c_t[:, 1:2], op0=ALU.mult, op1=ALU.add),
                     reads=[greB[jj], c_B, prmB], writes=[XB[j]])
            yt, yB = yrot()
            fns = []
            yreads = [wB]
            for jj in range(4):
                j = 4 * kc + jj
                cosB = cos5[:, jj, :]
                sinB = sin5[:, jj, :]
                tE, tF, tG, tH = tout[jj % 2]
                tEB, tFB, tGB, tHB = toutB[jj % 2]
                P.op("pool", lambda e, jj=jj, cosB=cosB, tE=tE: e.tensor_mul(out=tE, in0=gre[jj], in1=cosB), reads=[greB[jj], tabB], writes=[tEB])
                P.op("pool", lambda e, jj=jj, sinB=sinB, tF=tF: e.tensor_mul(out=tF, in0=gim[jj], in1=sinB), reads=[gimB[jj], tabB], writes=[tFB])
                P.op("pool", lambda e, jj=jj, cosB=cosB, tG=tG: e.tensor_mul(out=tG, in0=gim[jj], in1=cosB), reads=[gimB[jj], tabB], writes=[tGB])
                P.op("pool", lambda e, jj=jj, sinB=sinB, tH=tH: e.tensor_mul(out=tH, in0=gre[jj], in1=sinB), reads=[greB[jj], tabB], writes=[tHB])
                fns = [lambda e, yt=yt, j=j, tE=tE, jj=jj: e.matmul(yt[:, :], lhsT=CWr[:, j, :], rhs=tE, start=(jj == 0), stop=False),
                       lambda e, yt=yt, j=j, tF=tF: e.matmul(yt[:, :], lhsT=nCWr[:, j, :], rhs=tF, start=False, stop=False),
                       lambda e, yt=yt, j=j, tG=tG: e.matmul(yt[:, :], lhsT=CWi[:, j, :], rhs=tG, start=False, stop=False),
                       lambda e, yt=yt, j=j, tH=tH, jj=jj: e.matmul(yt[:, :], lhsT=CWi[:, j, :], rhs=tH, start=False, stop=(jj == 3))]
                P.op("pe", fns, reads=[wB, tEB, tFB, tGB, tHB], writes=[yB])
            P.op("dve", lambda e, yt=yt, usl=usl, kc=kc: e.scalar_tensor_tensor(out=ypre, in0=usl, scalar=dcol[:, kc:kc + 1], in1=yt[:, :], op0=ALU.mult, op1=ALU.add),
                 reads=[yB, usB, prmB], writes=[ypB])
            P.op("pool", lambda e: e.tensor_mul(out=gu, in0=ypre, in1=ypre), reads=[ypB], writes=[guB])
            P.op("pool", lambda e: e.tensor_scalar(out=gu, in0=gu, scalar1=0.044715, scalar2=1.0, op0=ALU.mult, op1=ALU.add), reads=[guB], writes=[guB])
            P.op("pool", lambda e: e.tensor_mul(out=gu, in0=gu, in1=ypre), reads=[guB, ypB], writes=[guB])
            P.op("act", lambda e: e.activation(out=sg, in_=gu, func=AF.Sigmoid, scale=1.5957691216057308), reads=[guB], writes=[sgB])
            P.op("dve", lambda e, kc=kc, s=s: e.tensor_mul(out=ygT[:, kc, s * 512:(s + 1) * 512], in0=sg, in1=ypre), reads=[sgB, ypB], writes=[ygB])
    P.barrier()
    A.p = p_main
    yo = [A.bf16(S) for _ in range(2)]
    yoB = [Buf("s5yo%d" % i) for i in range(2)]
    grot = psum_rot(k, [0, 1, 2, 3])
    for oc in range(4):
        y_t, y_B = yo[oc % 2], yoB[oc % 2]
        for s in range(NS):
            vt_, vB_ = grot()
            gt_, gB_ = grot()
            P.op("pe", [lambda e, vt_=vt_, kc=kc, oc=oc, s=s: e.matmul(vt_[:, :], lhsT=wglu[:, kc, oc * 128:(oc + 1) * 128], rhs=ygT[:, kc, s * 512:(s + 1) * 512],
                                                                     start=(kc == 0), stop=(kc == 3)) for kc in range(4)], reads=[wB, ygB], writes=[vB_])
            P.op("pe", [lambda e, gt_=gt_, kc=kc, oc=oc, s=s: e.matmul(gt_[:, :], lhsT=wglu[:, kc, 512 + oc * 128:512 + (oc + 1) * 128], rhs=ygT[:, kc, s * 512:(s + 1) * 512],
                                                                     start=(kc == 0), stop=(kc == 3)) for kc in range(4)], reads=[wB, ygB], writes=[gB_])
            P.op("act", lambda e, gt_=gt_: e.activation(out=sg, in_=gt_[:, :], func=AF.Sigmoid), reads=[gB_], writes=[sgB])
            P.op("dve", lambda e, vt_=vt_, y_t=y_t, s=s: e.tensor_mul(out=y_t[:, s * 512:(s + 1) * 512], in0=vt_[:, :], in1=sg), reads=[vB_, sgB], writes=[y_B])
        P.dma("sp", O["ybrT"][:, :, 10 + oc, :].rearrange("s p t -> p s t"), y_t.rearrange("p (s t) -> p s t", s=NS), k.sem("s5y%d" % (oc % 2)), reads=[y_B], writes=[k.DB["ybrT"]])
    P.barrier()


def stage_merge(k, l, cur, nxt):
    P, A, I, O = k.P, k.A, k.I, k.O
    A.p = k.const_end
    wbr = r3(A.bf16(14 * 1024), 14)
    wout = r3(A.bf16(8 * 1024), 8)
    wB = Buf("mw")
    P.dma("pool", wbr, I["w_branch"][l].rearrange("(kc p) n -> p kc n", p=128), k.sem("mw"), writes=[wB])
    P.dma("pool", wout, I["w_out"][l].rearrange("(kc p) n -> p kc n", p=128), k.sem("mw"), writes=[wB])
    ybr = [r3(A.bf16(14 * 512), 14) for _ in range(2)]
    ybrB = [Buf("ybr%d" % i) for i in range(2)]
    gt = r3(A.bf16(32 * 512), 32)
    gtB = Buf("gt")
    xs = [r3(A.f32(4096), 8) for _ in range(3)]
    xsB = [Buf("mxs%d" % i) for i in range(3)]
    mTs = [r3(A.bf16(8 * 512), 8) for _ in range(2)]
    mTBs = [Buf("mT%d" % i) for i in range(2)]
    mm = [A.f32(512) for _ in range(4)]
    mmB = [Buf("mm%d" % i) for i in range(4)]
    bbanks = [(k.pb[i], k.pbB[i]) for i in range(4)]
    orot = psum_rot(k, [4, 5, 6, 7])
    kcs = ((0, 4), (4, 6), (6, 10), (10, 14))
    ysrc = O["ybrT"]
    gsrc = O["gate"]
    xsrc = O[cur].rearrange("(kc p) t -> p kc t", p=128)
    xdst = O[nxt].rearrange("(kc p) t -> p kc t", p=128)
    for s in range(NS + 1):
        if s < NS:
            sl = slice(s * 512, (s + 1) * 512)
            y_t, y_B = ybr[s % 2], ybrB[s % 2]
            x_t, x_B = xs[s % 3], xsB[s % 3]
            mT, mTB = mTs[s % 2], mTBs[s % 2]
            P.dma("sp", y_t.rearrange("p a b -> p (a b)"), ysrc[s].rearrange("p a b -> p (a b)"), k.sem("my%d" % (s % 2)), reads=[k.DB["ybrT"]], writes=[y_B])
            P.dma("sp", gt.rearrange("p a b -> p (a b)"), gsrc[s].rearrange("p a b -> p (a b)"), k.sem("mg"), reads=[k.DB["gate"]], writes=[gtB])
            P.dma("sp", x_t, xsrc[:, :, sl], k.sem("mx%d" % (s % 3)), reads=[k.DB[cur]], writes=[x_B])
        if s >= 1:
            sp_ = s - 1
            px_t, px_B = xs[sp_ % 3], xsB[sp_ % 3]
            pmT, pmTB = mTs[sp_ % 2], mTBs[sp_ % 2]
        for co in range(8):
            if s < NS:
                for b in range(4):
                    pbt, pbB = bbanks[b]
                    k0, k1 = kcs[b]
                    P.op("pe", [lambda e, pbt=pbt, kc=kc, co=co, y_t=y_t, k0=k0, k1=k1: e.matmul(pbt[:, :], lhsT=wbr[:, kc, co * 128:(co + 1) * 128], rhs=y_t[:, kc, :],
                                                                                                start=(kc == k0), stop=(kc == k1 - 1)) for kc in range(k0, k1)],
                         reads=[wB, y_B], writes=[pbB])
            if s >= 1:
                dc = co
                obt, obB = orot()
                P.op("pe", [lambda e, obt=obt, c2=c2, dc=dc, pmT=pmT: e.matmul(obt[:, :], lhsT=wout[:, c2, dc * 128:(dc + 1) * 128], rhs=pmT[:, c2, :], start=(c2 == 0), stop=(c2 == 7))
                            for c2 in range(8)], reads=[wB, pmTB], writes=[obB])
            if s < NS:
                for b in range(4):
                    pbt, pbB = bbanks[b]
                    P.op("dve", lambda e, pbt=pbt, b=b, co=co: e.tensor_mul(out=mm[b], in0=pbt[:, :], in1=gt[:, b * 8 + co, :]), reads=[pbB, gtB], writes=[mmB[b]])
                P.op("dve", lambda e: e.tensor_add(out=mm[0], in0=mm[0], in1=mm[1]), reads=[mmB[1]], writes=[mmB[0]])
                P.op("dve", lambda e: e.tensor_add(out=mm[2], in0=mm[2], in1=mm[3]), reads=[mmB[3]], writes=[mmB[2]])
                P.op("dve", lambda e, co=co, mT=mT: e.tensor_add(out=mT[:, co, :], in0=mm[0], in1=mm[2]), reads=[mmB[0], mmB[2]], writes=[mTB])
            if s >= 1:
                P.op("dve", lambda e, obt=obt, dc=dc, px_t=px_t: e.tensor_add(out=px_t[:, dc, :], in0=obt[:, :], in1=px_t[:, dc, :]), reads=[obB], writes=[px_B])
        if s >= 1:
            psl = slice(sp_ * 512, (sp_ + 1) * 512)
            P.dma("sp", xdst[:, :, psl], px_t, k.sem("mo%d" % (sp_ % 3)), reads=[px_B], writes=[k.DB[nxt]])
    P.barrier()


def stage_ffn(k, l, cur, nxt):
    P, A, I, O = k.P, k.A, k.I, k.O
    A.p = k.const_end
    hnT = r3(A.bf16(8 * S), 8)
    hnB = [Buf("hn%d" % s) for s in range(NS)]
    p1 = A.p
    xs = [r3(A.f32(4096), 8) for _ in range(2)]
    xsB = [Buf("fxs%d" % i) for i in range(2)]
    sq = r3(A.bf16(4096), 8)
    sqB = Buf("fsq")
    rt = A.f32(512)
    rtB = Buf("frt")
    src = O[cur].rearrange("(kc p) t -> p kc t", p=128)
    for s in range(NS):
        P.dma("sp", xs[s % 2], src[:, :, s * 512:(s + 1) * 512], k.sem("fx%d" % (s % 2)), reads=[k.DB[cur]], writes=[xsB[s % 2]])
        rmsnorm_supertile(k, xs[s % 2], xsB[s % 2], 4 + l, hnT[:, :, s * 512:(s + 1) * 512], hnB[s], sq, sqB, rt, rtB, (k.pb[7], k.pbB[7]))
    P.barrier()
    A.p = p1
    wg = [r3(A.bf16(8 * 512), 8) for _ in range(2)]
    wu = [r3(A.bf16(8 * 512), 8) for _ in range(2)]
    wgB = [Buf("wg%d" % i) for i in range(2)]
    hrow = [A.bf16(S) for _ in range(2)]
    hrowB = [Buf("hrow%d" % i) for i in range(2)]
    sg = [A.f32(512) for _ in range(2)]
    sgB = [Buf("fsg%d" % i) for i in range(2)]
    wup = I["w_up"][l].rearrange("(kc p) n -> p kc n", p=128)
    rot = psum_rot(k, [0, 1, 2, 3, 4, 5])
    si = 0
    hi = 0
    ei = 0
    for c0 in range(0, FH, 512):
        cw = min(512, FH - c0)
        w_g, w_u, w_B = wg[si % 2], wu[si % 2], wgB[si % 2]
        P.dma("pool", w_g[:, :, 0:cw], wup[:, :, c0:c0 + cw], k.sem("fw%d" % (si % 2)), writes=[w_B])
        P.dma("pool", w_u[:, :, 0:cw], wup[:, :, FH + c0:FH + c0 + cw], k.sem("fw%d" % (si % 2)), writes=[w_B])
        si += 1
        for cc in range(cw // 128):
            h_t, h_B = hrow[hi % 2], hrowB[hi % 2]
            for s in range(NS):
                gt_, gB_ = rot()
                ut_, uB_ = rot()
                P.op("pe", [lambda e, gt_=gt_, kc=kc, w_g=w_g, cc=cc, s=s: e.matmul(gt_[:, :], lhsT=w_g[:, kc, cc * 128:(cc + 1) * 128], rhs=hnT[:, kc, s * 512:(s + 1) * 512],
                                                                                   start=(kc == 0), stop=(kc == 7)) for kc in range(8)], reads=[w_B, hnB[s]], writes=[gB_])
                P.op("pe", [lambda e, ut_=ut_, kc=kc, w_u=w_u, cc=cc, s=s: e.matmul(ut_[:, :], lhsT=w_u[:, kc, cc * 128:(cc + 1) * 128], rhs=hnT[:, kc, s * 512:(s + 1) * 512],
                                                                                   start=(kc == 0), stop=(kc == 7)) for kc in range(8)], reads=[w_B, hnB[s]], writes=[uB_])
                s_t, s_B = sg[ei % 2], sgB[ei % 2]
                ei += 1
                P.op("act", lambda e, gt_=gt_, s_t=s_t: e.activation(out=s_t, in_=gt_[:, :], func=AF.Silu), reads=[gB_], writes=[s_B])
                P.op("dve", lambda e, ut_=ut_, s_t=s_t, h_t=h_t, s=s: e.tensor_mul(out=h_t[:, s * 512:(s + 1) * 512], in0=ut_[:, :], in1=s_t), reads=[uB_, s_B], writes=[h_B])
            r0 = c0 + cc * 128
            P.dma("sp", O["hT"][r0:r0 + 128, :], h_t, k.sem("fh%d" % (hi % 2)), reads=[h_B], writes=[k.DB["hT"]])
            hi += 1
    P.barrier()
    A.p = k.const_end
    wdn = r3(A.bf16(22 * 1024), 22)
    wdB = Buf("wdn")
    P.dma("pool", wdn, I["w_down"][l].rearrange("(kc p) n -> p kc n", p=128), k.sem("fwd"), writes=[wdB])
    hs = [r3(A.bf16(22 * 512), 22) for _ in range(2)]
    hsB = [Buf("hs%d" % i) for i in range(2)]
    xs = [r3(A.f32(4096), 8) for _ in range(2)]
    xsB = [Buf("dxs%d" % i) for i in range(2)]
    hsrc = O["hT"].rearrange("(kc p) t -> p kc t", p=128)
    xdst = O[nxt].rearrange("(kc p) t -> p kc t", p=128)
    rot = psum_rot(k, [0, 1, 2, 3])
    for s in range(NS):
        sl = slice(s * 512, (s + 1) * 512)
        h_t, h_B = hs[s % 2], hsB[s % 2]
        x_t, x_B = xs[s % 2], xsB[s % 2]
        P.dma("sp", h_t, hsrc[:, :, sl], k.sem("dh%d" % (s % 2)), reads=[k.DB["hT"]], writes=[h_B])
        P.dma("sp", x_t, src[:, :, sl], k.sem("dx%d" % (s % 2)), reads=[k.DB[cur]], writes=[x_B])
        for dc in range(8):
            pbt, pbB = rot()
            P.op("pe", [lambda e, pbt=pbt, kc=kc, dc=dc, h_t=h_t: e.matmul(pbt[:, :], lhsT=wdn[:, kc, dc * 128:(dc + 1) * 128], rhs=h_t[:, kc, :], start=(kc == 0), stop=(kc == 21))
                        for kc in range(22)], reads=[wdB, h_B], writes=[pbB])
            P.op("dve", lambda e, pbt=pbt, dc=dc, x_t=x_t: e.tensor_add(out=x_t[:, dc, :], in0=pbt[:, :], in1=x_t[:, dc, :]), reads=[pbB], writes=[x_B])
        P.dma("sp", xdst[:, :, sl], x_t, k.sem("do%d" % (s % 2)), reads=[x_B], writes=[k.DB[nxt]])
    P.barrier()


def stage_final(k, cur):
    P, A, I, O = k.P, k.A, k.I, k.O
    A.p = k.const_end
    xs = [r3(A.f32(4096), 8) for _ in range(2)]
    xsB = [Buf("zxs%d" % i) for i in range(2)]
    xn = r3(A.f32(4096), 8)
    xnB = Buf("zxn")
    sq = r3(A.bf16(4096), 8)
    sqB = Buf("zsq")
    rt = A.f32(512)
    rtB = Buf("zrt")
    ot = [A.f32(1024) for _ in range(2)]
    otB = [Buf("zot%d" % i) for i in range(2)]
    src = O[cur].rearrange("(kc p) t -> p kc t", p=128)
    rot = psum_rot(k, [0, 1, 2, 3])
    for s in range(NS):
        P.dma("sp", xs[s % 2], src[:, :, s * 512:(s + 1) * 512], k.sem("zx%d" % (s % 2)), reads=[k.DB[cur]], writes=[xsB[s % 2]])
        rmsnorm_supertile(k, xs[s % 2], xsB[s % 2], 8, xn, xnB, sq, sqB, rt, rtB, (k.pb[7], k.pbB[7]), out_engines=("dve",))
        for tt in range(4):
            t = s * 4 + tt
            o_t, o_B = ot[t % 2], otB[t % 2]
            for half in range(2):
                pbt, pbB = rot()
                P.op("pe", [lambda e, pbt=pbt, q=q, half=half, tt=tt: e.transpose(out=pbt[:, q * 128:(q + 1) * 128], in_=xn[:, half * 4 + q, tt * 128:(tt + 1) * 128], identity=k.ident_f)
                            for q in range(4)], reads=[xnB, k.CB], writes=[pbB])
                if half == 0:
                    P.op("dve", lambda e, pbt=pbt, o_t=o_t: e.tensor_copy(out=o_t[:, 0:512], in_=pbt[:, :]), reads=[pbB], writes=[o_B])
                else:
                    P.op("act", lambda e, pbt=pbt, o_t=o_t: e.activation(out=o_t[:, 512:1024], in_=pbt[:, :], func=AF.Copy), reads=[pbB], writes=[o_B])
            P.dma("sp", O["out"][t * 128:(t + 1) * 128, :], o_t, k.sem("zo%d" % (t % 2)), reads=[o_B])
    P.barrier()


_PROG_CACHE = {}


def make_in_maps(inputs, ncores=8):
    consts = build_consts()
    tabs = build_dil_tables(np.asarray(inputs["rel_bias"], np.float32))
    s5l = s5_layouts({kk: np.asarray(inputs[kk], np.float32) for kk in
                      ("s5_a_re", "s5_a_im", "s5_log_dt", "s5_b_re", "s5_b_im", "s5_c_re", "s5_c_im")})
    shared = {}
    for nm in ("attn_norm_g", "w_in", "pool_w", "pool_scale", "s5_d", "s5_w_glu", "w_branch", "w_gate", "w_out",
               "ffn_norm_g", "w_up", "w_down", "final_norm_g"):
        shared[nm] = np.ascontiguousarray(np.asarray(inputs[nm], np.float32))
    shared["dil_tab"] = tabs
    shared.update(consts)
    shared.update(s5l)
    x = np.asarray(inputs["x"], np.float32)
    maps = []
    for c in range(ncores):
        m = dict(shared)
        m["x"] = np.ascontiguousarray(x[c])
        maps.append(m)
    return maps


def kernel(**inputs):
    if "full" not in _PROG_CACHE:
        _PROG_CACHE["full"] = build_program(NL)
    nc = _PROG_CACHE["full"]
    maps = make_in_maps(inputs, 8)
    res = run_bass_kernel_spmd(nc, maps, core_ids=list(range(8)))
    out = np.stack([np.asarray(r["out"], np.float32) for r in res.results], 0)
    return out
```
